# Optimizing a Trainium2 kernel written in Bass

```python
import jax, jax.numpy as jnp
from jax import lax
import numpy as np

D_MODEL = 2048
BATCH = 8
SEQ = 2048
DEPTH = 4

HEAD_DIM = 128
DIL_GROUPS = ((128, 1), (512, 4), (2048, 16))
HEADS_PER_GROUP = 4
N_ATTN_HEADS = HEADS_PER_GROUP * len(DIL_GROUPS)
ATTN_WIDTH = N_ATTN_HEADS * HEAD_DIM
ATTN_OUT_WIDTH = HEADS_PER_GROUP * HEAD_DIM
BAND_BLOCK = 64
NEG_INF = -1e30
CONV_WIDTH = D_MODEL // 2
CONV_KERNEL = 31
N_MEM = 256
X_HEADS = 4
X_HEAD_DIM = D_MODEL // X_HEADS
D_FF = -(-8 * D_MODEL // (3 * 256)) * 256
IN_WIDTH = 2 * CONV_WIDTH + 3 * ATTN_WIDTH + 2 * D_MODEL
EPS = 1e-6

kernel_name = "hybrid_conformer_dilated_encoder"


def rms_norm(x, g):
    xf = x.astype(jnp.float32)
    y = xf * lax.rsqrt(jnp.mean(xf * xf, axis=-1, keepdims=True) + EPS)
    return (y * g.astype(jnp.float32)).astype(x.dtype)


def layer_norm(x, g, b):
    xf = x.astype(jnp.float32)
    mu = jnp.mean(xf, axis=-1, keepdims=True)
    var = jnp.mean(jnp.square(xf - mu), axis=-1, keepdims=True)
    y = (xf - mu) * lax.rsqrt(var + EPS)
    return (y * g.astype(jnp.float32) + b.astype(jnp.float32)).astype(x.dtype)


def alibi_slopes(n):
    return jnp.exp2(-8.0 * (jnp.arange(n, dtype=jnp.float32) + 1.0) / n)


def conformer_conv_branch(u, w_dw, b_dw, ln_g, ln_b):
    a, gate = jnp.split(u, 2, axis=-1)
    h = a * jax.nn.sigmoid(gate)
    pad = CONV_KERNEL // 2
    h = lax.conv_general_dilated(
        h, w_dw[:, None, :].astype(h.dtype), window_strides=(1,), padding=[(pad, pad)],
        dimension_numbers=("NWC", "WIO", "NWC"), feature_group_count=h.shape[-1]) + b_dw
    h = layer_norm(h, ln_g, ln_b)
    return jax.nn.silu(h)


def dilated_window_attention(q, k, v, window, dilation, slopes):
    B, S, H, Dh = q.shape
    half = window // (2 * dilation)
    L = S // dilation
    nb = -(-L // BAND_BLOCK)
    Lp = nb * BAND_BLOCK

    def to_sub(t):
        return t.reshape(B, L, dilation, H, Dh).transpose(0, 2, 3, 1, 4).astype(jnp.float32)

    qs = jnp.pad(to_sub(q), ((0, 0), (0, 0), (0, 0), (0, Lp - L), (0, 0)))
    kv_pad = ((0, 0), (0, 0), (0, 0), (BAND_BLOCK, Lp - L + BAND_BLOCK), (0, 0))
    kp = jnp.pad(to_sub(k), kv_pad)
    vp = jnp.pad(to_sub(v), kv_pad)
    qb = qs.reshape(B, dilation, H, nb, BAND_BLOCK, Dh)

    def band(t):
        return jnp.concatenate(
            [t[:, :, :, o:o + Lp].reshape(B, dilation, H, nb, BAND_BLOCK, Dh)
             for o in (0, BAND_BLOCK, 2 * BAND_BLOCK)], axis=4)

    kb = band(kp)
    vb = band(vp)
    jq = jnp.arange(Lp).reshape(nb, BAND_BLOCK)
    jk = (jnp.arange(nb)[:, None] * BAND_BLOCK
          + jnp.arange(3 * BAND_BLOCK)[None, :] - BAND_BLOCK)
    rel = jk[:, None, :] - jq[:, :, None]
    valid = (jnp.abs(rel) <= half) & (jk[:, None, :] >= 0) & (jk[:, None, :] < L)
    dist = (dilation * jnp.abs(rel)).astype(jnp.float32)

    s = jnp.einsum("brhnqd,brhnkd->brhnqk", qb, kb) * (Dh ** -0.5)
    s = s - slopes[:, None, None, None] * dist
    s = jnp.where(valid, s, NEG_INF)
    m = jnp.max(s, axis=-1, keepdims=True)
    p = jnp.exp(s - m)
    den = jnp.sum(p, axis=-1, keepdims=True)
    o = jnp.einsum("brhnqk,brhnkd->brhnqd", p, vb) / den
    lse = (m + jnp.log(den))[..., 0]

    o = o.reshape(B, dilation, H, Lp, Dh)[:, :, :, :L].transpose(0, 3, 1, 2, 4).reshape(B, S, H, Dh)
    lse = lse.reshape(B, dilation, H, Lp)[:, :, :, :L].transpose(0, 3, 1, 2).reshape(B, S, H)
    return o, lse


def dilated_mixture_branch(q, k, v):
    B, S, _ = q.shape
    q = q.reshape(B, S, N_ATTN_HEADS, HEAD_DIM)
    k = k.reshape(B, S, N_ATTN_HEADS, HEAD_DIM)
    v = v.reshape(B, S, N_ATTN_HEADS, HEAD_DIM)
    slopes = alibi_slopes(N_ATTN_HEADS)
    outs, lses = [], []
    for g, (window, dilation) in enumerate(DIL_GROUPS):
        hs = slice(g * HEADS_PER_GROUP, (g + 1) * HEADS_PER_GROUP)
        o, lse = dilated_window_attention(q[:, :, hs], k[:, :, hs], v[:, :, hs],
                                          window, dilation, slopes[hs])
        outs.append(o)
        lses.append(lse)
    alpha = jax.nn.softmax(jnp.stack(lses, axis=0), axis=0)
    y = jnp.sum(alpha[..., None] * jnp.stack(outs, axis=0), axis=0)
    return y.reshape(B, S, ATTN_OUT_WIDTH).astype(v.dtype)


def memory_cross_attention(h, memn, w_cq, w_ck, w_cv, w_co):
    B, S, _ = h.shape
    M = memn.shape[1]
    q = (h @ w_cq).reshape(B, S, X_HEADS, X_HEAD_DIM)
    k = (memn @ w_ck).reshape(B, M, X_HEADS, X_HEAD_DIM)
    v = (memn @ w_cv).reshape(B, M, X_HEADS, X_HEAD_DIM)
    s = jnp.einsum("bshd,bmhd->bhsm", q.astype(jnp.float32), k.astype(jnp.float32)) * (X_HEAD_DIM ** -0.5)
    p = jax.nn.softmax(s, axis=-1)
    o = jnp.einsum("bhsm,bmhd->bshd", p, v.astype(jnp.float32)).astype(h.dtype)
    return o.reshape(B, S, D_MODEL) @ w_co


def swiglu_ffn(h, w_gate, w_up, w_down):
    return (jax.nn.silu(h @ w_gate) * (h @ w_up)) @ w_down


def setup_inputs(seed: int = 0) -> dict:
    key = jax.random.key(seed)
    ks = jax.random.split(key, 24)
    f32 = jnp.float32

    def w(k, shape, fan_in):
        return jax.random.normal(k, shape, f32) * (fan_in ** -0.5)

    def gain(k, shape):
        return 1.0 + 0.02 * jax.random.normal(k, shape, f32)

    def bias(k, shape):
        return 0.02 * jax.random.normal(k, shape, f32)

    L = DEPTH
    return {
        "x": jax.random.normal(ks[0], (BATCH, SEQ, D_MODEL), f32),
        "mem": jax.random.normal(ks[1], (BATCH, N_MEM, D_MODEL), f32),
        "w_in": w(ks[2], (L, D_MODEL, IN_WIDTH), D_MODEL),
        "w_dw": w(ks[3], (L, CONV_KERNEL, CONV_WIDTH), CONV_KERNEL),
        "b_dw": bias(ks[4], (L, CONV_WIDTH)),
        "conv_ln_g": gain(ks[5], (L, CONV_WIDTH)),
        "conv_ln_b": bias(ks[6], (L, CONV_WIDTH)),
        "w_conv_out": w(ks[7], (L, CONV_WIDTH, D_MODEL), CONV_WIDTH),
        "w_attn_proj": w(ks[8], (L, ATTN_OUT_WIDTH, D_MODEL), ATTN_OUT_WIDTH),
        "w_o": w(ks[9], (L, D_MODEL, D_MODEL), D_MODEL),
        "g_mix_pre": gain(ks[10], (L, D_MODEL)),
        "g_mix_post": gain(ks[11], (L, D_MODEL)),
        "g_mem": gain(ks[12], (L, D_MODEL)),
        "w_cq": w(ks[13], (L, D_MODEL, D_MODEL), D_MODEL),
        "w_ck": w(ks[14], (L, D_MODEL, D_MODEL), D_MODEL),
        "w_cv": w(ks[15], (L, D_MODEL, D_MODEL), D_MODEL),
        "w_co": w(ks[16], (L, D_MODEL, D_MODEL), D_MODEL),
        "g_x_pre": gain(ks[17], (L, D_MODEL)),
        "g_x_post": gain(ks[18], (L, D_MODEL)),
        "w_ffn_gate": w(ks[19], (L, D_MODEL, D_FF), D_MODEL),
        "w_ffn_up": w(ks[20], (L, D_MODEL, D_FF), D_MODEL),
        "w_ffn_down": w(ks[21], (L, D_FF, D_MODEL), D_FF),
        "g_ffn_pre": gain(ks[22], (L, D_MODEL)),
        "g_ffn_post": gain(ks[23], (L, D_MODEL)),
    }


def reference(x, mem, w_in, w_dw, b_dw, conv_ln_g, conv_ln_b, w_conv_out, w_attn_proj, w_o,
              g_mix_pre, g_mix_post, g_mem, w_cq, w_ck, w_cv, w_co, g_x_pre, g_x_post,
              w_ffn_gate, w_ffn_up, w_ffn_down, g_ffn_pre, g_ffn_post):
    c0 = 2 * CONV_WIDTH
    c1 = c0 + ATTN_WIDTH
    c2 = c1 + ATTN_WIDTH
    c3 = c2 + ATTN_WIDTH
    c4 = c3 + D_MODEL
    for l in range(DEPTH):
        h = rms_norm(x, g_mix_pre[l])
        z = h @ w_in[l]
        u_conv, q, k, v, ga, gb = jnp.split(z, [c0, c1, c2, c3, c4], axis=-1)
        y_conv = conformer_conv_branch(u_conv, w_dw[l], b_dw[l], conv_ln_g[l], conv_ln_b[l]) @ w_conv_out[l]
        y_attn = dilated_mixture_branch(q, k, v) @ w_attn_proj[l]
        merged = jax.nn.sigmoid(ga) * y_conv + jax.nn.sigmoid(gb) * y_attn
        x = x + rms_norm(merged @ w_o[l], g_mix_post[l])
        h = rms_norm(x, g_x_pre[l])
        memn = rms_norm(mem, g_mem[l])
        x = x + rms_norm(memory_cross_attention(h, memn, w_cq[l], w_ck[l], w_cv[l], w_co[l]), g_x_post[l])
        h = rms_norm(x, g_ffn_pre[l])
        x = x + rms_norm(swiglu_ffn(h, w_ffn_gate[l], w_ffn_up[l], w_ffn_down[l]), g_ffn_post[l])
    return x
```

```python
import contextlib
import numpy as np
import concourse.bass as bass
import concourse.mybir as mybir
from concourse.bass_utils import run_bass_kernel_spmd

F32 = mybir.dt.float32
BF16 = mybir.dt.bfloat16
AF = mybir.ActivationFunctionType
ALU = mybir.AluOpType

D = 2048
SEQ = 2048
NMEM = 256
DEPTH = 4
DFF = 5632
NCH_IN = 84
EPS = 1e-6
NVEC = 384

ENGS = ("pe", "act", "dve", "pool", "sp")
N_DMA_SEMS = 12


class Op:
    __slots__ = ("eng", "fn", "deps", "signal", "val", "is_dma", "sem", "epoch", "extra_wait")

    def __init__(self, eng, fn, is_dma, epoch):
        self.eng = eng
        self.fn = fn
        self.deps = []
        self.signal = is_dma
        self.val = None
        self.is_dma = is_dma
        self.sem = None
        self.epoch = epoch
        self.extra_wait = None


class Sched:
    def __init__(self):
        self.ops = {e: [] for e in ENGS}
        self.last_w = {}
        self.readers = {}
        self.epoch = 0

    def new_epoch(self):
        self.epoch += 1

    def op(self, eng, fn, reads=(), writes=(), dma=False):
        o = Op(eng, fn, dma, self.epoch)
        deps = {}
        for r in reads:
            w = self.last_w.get(r)
            if w is not None:
                deps[id(w)] = w
        for r in writes:
            w = self.last_w.get(r)
            if w is not None:
                deps[id(w)] = w
            for rd in self.readers.get(r, ()):
                deps[id(rd)] = rd
        for d in deps.values():
            if d is o:
                continue
            if d.eng == "pe" and eng == "pe" and not d.is_dma and not dma:
                continue
            d.signal = True
            o.deps.append(d)
        for r in writes:
            self.last_w[r] = o
            self.readers[r] = []
        for r in reads:
            self.readers.setdefault(r, []).append(o)
        self.ops[eng].append(o)
        return o

    def emit(self, nc, stack):
        nep = self.epoch + 1
        esem = {}
        for e in ENGS:
            for ep in range(nep):
                esem[(e, ep)] = stack.enter_context(nc.semaphore(f"s_{e}_{ep}"))
        dsem = {}
        for e in ("sp", "pool", "act"):
            dsem[e] = [stack.enter_context(nc.semaphore(f"d_{e}_{i}")) for i in range(N_DMA_SEMS)]
        for e in ENGS:
            cnt = {}
            k = 0
            for o in self.ops[e]:
                if o.is_dma:
                    o.sem = dsem[e][k % N_DMA_SEMS]
                    o.val = 16 * (k // N_DMA_SEMS + 1)
                    if k >= N_DMA_SEMS:
                        o.extra_wait = (o.sem, 16 * (k // N_DMA_SEMS))
                    k += 1
                elif o.signal:
                    c = cnt.get(o.epoch, 0) + 1
                    cnt[o.epoch] = c
                    o.val = c
                    o.sem = esem[(e, o.epoch)]
        block = stack.enter_context(nc.Block())

        def replay(e):
            def body(eng):
                seen = {}
                for o in self.ops[e]:
                    waits = []
                    if o.extra_wait is not None:
                        waits.append(o.extra_wait)
                    for d in o.deps:
                        waits.append((d.sem, d.val))
                    for (s, v) in waits:
                        key = id(s)
                        if seen.get(key, 0) >= v:
                            continue
                        seen[key] = v
                        eng.wait_ge(s, v)
                    inst = o.fn(eng)
                    if o.signal:
                        inst.then_inc(o.sem, 16 if o.is_dma else 1)
            return body

        block.tensor(replay("pe"))
        block.scalar(replay("act"))
        block.vector(replay("dve"))
        block.gpsimd(replay("pool"))
        block.sync(replay("sp"))


KB = 1024
HA_OFF = 0
BIG_OFF = 64 * KB
RING_OFF = BIG_OFF + 88 * KB
NSLOT = 4
SCR_OFF = RING_OFF + NSLOT * 4 * KB
SCR_SIZE = 28 * KB
ARENA_BYTES = SCR_OFF + SCR_SIZE


def mask_index(hd, o):
    g = hd // 4
    if g < 2:
        return hd * 3 + (o + 1)
    return 24 + (hd - 8)


def build_program(depth=DEPTH, dbg=None):
    nc = bass.Bass("TRN2", target_bir_lowering=False)
    L = depth

    def dram(name, shape, kind):
        return nc.dram_tensor(name, list(shape), F32, kind=kind).ap()

    xT_in = dram("xT", [8, 128, 16 * 256], "ExternalInput")
    memT_in = dram("memT", [D, NMEM], "ExternalInput")
    w_in_t = dram("w_in", [L * 84, 128, 16, 128], "ExternalInput")
    w_co_t = dram("w_conv_out", [L * 16, 128, 8, 128], "ExternalInput")
    w_ap_t = dram("w_attn_proj", [L * 16, 128, 4, 128], "ExternalInput")
    w_o_t = dram("w_o", [L * 16, 128, 16, 128], "ExternalInput")
    w_cq_t = dram("w_cq", [L * 16, 128, 16, 128], "ExternalInput")
    w_ck_t = dram("w_ck", [L * 16, 128, 16, 128], "ExternalInput")
    w_cv_t = dram("w_cv", [L * 16, 128, 16, 128], "ExternalInput")
    w_cx_t = dram("w_co", [L * 16, 128, 16, 128], "ExternalInput")
    w_fg_t = dram("w_ffn_gate", [L * 44, 128, 16, 128], "ExternalInput")
    w_fu_t = dram("w_ffn_up", [L * 44, 128, 16, 128], "ExternalInput")
    w_fd_t = dram("w_ffn_down", [L * 16, 128, 44, 128], "ExternalInput")
    vecs_in = dram("vecs", [128, L * NVEC], "ExternalInput")
    mask_in = dram("masks", [128, 29 * 128], "ExternalInput")
    outT = dram("outT", [8, 128, 16 * 256], "ExternalOutput")
    xs = dram("xs_scratch", [8, 128, 16 * 256], "Internal")
    ys = dram("ys_scratch", [8, 128, 16 * 256], "Internal")

    S = Sched()
    st = contextlib.ExitStack()
    with st:
        arena = st.enter_context(nc.sbuf_tensor("arena", [128, ARENA_BYTES // 2], BF16))
        VEC = st.enter_context(nc.sbuf_tensor("vec", [128, L * NVEC], F32))
        ONES = st.enter_context(nc.sbuf_tensor("ones", [128, 128], BF16))
        IDENT = st.enter_context(nc.sbuf_tensor("ident", [128, 128], BF16))
        EPST = st.enter_context(nc.sbuf_tensor("epst", [128, 1], F32))
        PS = st.enter_context(nc.psum_tensor("ps", [128, 4096], F32))

        def bank(b, n=512, off=0):
            return PS[:, b * 512 + off: b * 512 + off + n]

        def pk(*bs):
            return [("ps", b) for b in bs]

        def av(off, nbytes, dt=BF16):
            lo = off // 2
            v = arena[:, lo: lo + nbytes // 2]
            if dt == F32:
                v = v.bitcast(F32)
            keys = [("A", k) for k in range(off // KB, (off + nbytes - 1) // KB + 1)]
            return v, keys

        HA = arena[:, 0: 32768].rearrange("p (c n) -> p c n", c=16)

        def ha_keys(t0=0, t1=SEQ):
            return [("HA", k) for k in range(t0 // 256, (t1 - 1) // 256 + 1)]

        ring = []
        for s_ in range(NSLOT):
            v, k = av(RING_OFF + s_ * 4 * KB, 4 * KB)
            ring.append((v.rearrange("p (k c) -> p k c", k=16), k))
        ring_ctr = [0]

        def load_piece(src):
            kcn = src.shape[1]
            slot = ring_ctr[0] % NSLOT
            ring_ctr[0] += 1
            v, k = ring[slot]
            dst = v[:, 0:kcn, :]
            S.op("pool", lambda e, dst=dst, src=src: e.dma_start(out=dst, in_=src), writes=k, dma=True)
            return dst, k

        dump_names = []

        def dump(name, view, keys, dt=BF16):
            if not _DUMP:
                return
            shp = list(view.shape)
            t = nc.dram_tensor("dump_" + name, shp, dt, kind="ExternalOutput").ap()
            dump_names.append("dump_" + name)
            S.op("sp", lambda e: e.dma_start(out=t, in_=view), reads=keys, writes=[("dump", name)], dma=True)

        S.op("sp", lambda e: e.dma_start(out=VEC[:], in_=vecs_in), writes=["vec"], dma=True)
        S.op("dve", lambda e: e.memset(ONES[:], 1.0), writes=["ones"])
        S.op("dve", lambda e: e.memset(EPST[:], EPS), writes=["eps"])
        S.op("pool", lambda e: e.dma_start(out=IDENT[:], in_=mask_in[:, 28 * 128:29 * 128]), writes=["ident"], dma=True)

        def vec(l, idx, c):
            o = l * NVEC + idx * 16 + c
            return VEC[:, o:o + 1]

        def cvec(l, idx, cc):
            o = l * NVEC + 112 + idx * 8 + cc
            return VEC[:, o:o + 1]

        def wdw(l, cc):
            o = l * NVEC + 136 + cc * 31
            return VEC[:, o:o + 31]

        G_MIX_PRE, G_MIX_POST, G_MEM, G_X_PRE, G_X_POST, G_FFN_PRE, G_FFN_POST = range(7)

        bank_rr = [0]

        def next_bank():
            b = bank_rr[0] % 8
            bank_rr[0] += 1
            return b

        def rstd_from_bank(b, n, dst, dkeys, scale):
            S.op("act", lambda e: e.activation(out=dst, in_=bank(b, n), func=AF.Sqrt, scale=scale, bias=EPST[:]),
                 reads=pk(b) + ["eps"], writes=dkeys)
            S.op("dve", lambda e: e.reciprocal(out=dst, in_=dst), reads=dkeys, writes=dkeys)

        NTU = 256
        NT_U = SEQ // NTU

        UP = (BIG_OFF + 40 * KB, BIG_OFF + 56 * KB, BIG_OFF + 72 * KB)
        LO = (BIG_OFF, BIG_OFF + 16 * KB, BIG_OFF + 32 * KB)
        TOP = (BIG_OFF + 48 * KB, BIG_OFF + 64 * KB, BIG_OFF + 80 * KB)

        def reg_disjoint(r1, r2):
            iv1 = [(r1[0], 16 * KB), (r1[1], 16 * KB), (r1[2], 8 * KB)]
            iv2 = [(r2[0], 16 * KB), (r2[1], 16 * KB), (r2[2], 8 * KB)]
            for (a, la) in iv1:
                for (b, lb) in iv2:
                    if a < b + lb and b < a + la:
                        return False
            return True

        def update_stages(l, x_src, has_y, g_post, x_dst, g_pre, final=False, l_pre=None, regs=None, add_eng="pool"):
            l_pre = l if l_pre is None else l_pre
            xdv = x_dst
            if regs is None:
                regs = [UP, LO] * 4

            def bufs(t):
                xo, yo, so = regs[t]
                XT, xk = av(xo, 16 * KB, F32)
                YT, yk = av(yo, 16 * KB, F32)
                SQ, sk = av(so, 8 * KB)
                return (XT.rearrange("p (c n) -> p c n", c=16), xk, YT.rearrange("p (c n) -> p c n", c=16), yk,
                        SQ.rearrange("p (c n) -> p c n", c=16), sk)

            RY, ryk = av(SCR_OFF + 26 * KB, KB, F32)
            RX, rxk = av(SCR_OFF + 27 * KB, KB, F32)

            def st_L(t):
                XT, xk, YT, yk, SQ, sk = bufs(t)
                half = 0 if t * NTU < 1024 else 1
                xsrc_keys = [("xs", t)] if x_src is xs else []
                S.op("sp", lambda e: e.dma_start(out=XT, in_=x_src[t].rearrange("p (c n) -> p c n", c=16)),
                     reads=xsrc_keys, writes=xk, dma=True)
                if has_y:
                    S.op("sp", lambda e: e.dma_start(out=YT, in_=ys[t].rearrange("p (c n) -> p c n", c=16)),
                         reads=[("ys", m, half) for m in range(16)], writes=yk, dma=True)

            def st_A(t):
                XT, xk, YT, yk, SQ, sk = bufs(t)
                if has_y:
                    S.op("act", lambda e: e.activation(out=SQ, in_=YT, func=AF.Square), reads=yk, writes=sk)
                else:
                    S.op("act", lambda e: e.activation(out=SQ, in_=XT, func=AF.Square), reads=xk, writes=sk)

            def st_B(t):
                XT, xk, YT, yk, SQ, sk = bufs(t)
                b = next_bank()

                def ssq(e):
                    for c in range(16):
                        i = e.matmul(bank(b, NTU), lhsT=ONES[:], rhs=SQ[:, c, :], start=(c == 0), stop=(c == 15))
                    return i
                S.op("pe", ssq, reads=sk + ["ones"], writes=pk(b))
                rstd_from_bank(b, NTU, RY, ryk, 1.0 / D)

                def scale_y(e):
                    for c in range(16):
                        i = e.scalar_tensor_tensor(out=YT[:, c, :], in0=YT[:, c, :], scalar=vec(l, g_post, c),
                                                   in1=RY, op0=ALU.mult, op1=ALU.mult)
                    return i
                S.op("dve", scale_y, reads=yk + ryk + ["vec"], writes=yk)
                S.op(add_eng(t) if callable(add_eng) else add_eng, lambda e: e.tensor_tensor(out=XT, in0=XT, in1=YT, op=ALU.add),
                     reads=xk + yk, writes=xk)
                if xdv is not None:
                    dkeys = [("xs", t)] if x_dst is xs else [("out", t)]
                    S.op("sp", lambda e: e.dma_start(out=xdv[t].rearrange("p (c n) -> p c n", c=16), in_=XT),
                         reads=xk, writes=dkeys, dma=True)
                if not final:
                    S.op("act", lambda e: e.activation(out=SQ, in_=XT, func=AF.Square), reads=xk, writes=sk)

            def st_C(t):
                XT, xk, YT, yk, SQ, sk = bufs(t)
                t0 = t * NTU
                b2 = next_bank()

                def ssq2(e):
                    for c in range(16):
                        i = e.matmul(bank(b2, NTU), lhsT=ONES[:], rhs=SQ[:, c, :], start=(c == 0), stop=(c == 15))
                    return i
                S.op("pe", ssq2, reads=sk + ["ones"], writes=pk(b2))
                rstd_from_bank(b2, NTU, RX, rxk, 1.0 / D)

                def mk_h(e):
                    for c in range(16):
                        i = e.scalar_tensor_tensor(out=HA[:, c, t0:t0 + NTU], in0=XT[:, c, :], scalar=vec(l_pre, g_pre, c),
                                                   in1=RX, op0=ALU.mult, op1=ALU.mult)
                    return i
                S.op("dve", mk_h, reads=xk + rxk + ["vec"], writes=ha_keys(t0, t0 + NTU))

            stages = {}
            for t in range(NT_U):
                mk = lambda f, t=t: (lambda: f(t))
                stages[t] = [mk(st_L), mk(st_A), (mk(st_B) if has_y else None), (None if final else mk(st_C))]
            return stages, regs

        def sched_tiles(ts, stages, regs, lat=(3, 1, 3, 1)):
            ts = list(ts)
            stage = {t: 0 for t in ts}
            ready = {t: 0 for t in ts}
            hooks = []
            h = 0
            while any(stage[t] < 4 for t in ts):
                cur = []
                for t in ts:
                    while stage[t] < 4:
                        s_ = stage[t]
                        if s_ == 0 and any(stage[u] < 4 and not reg_disjoint(regs[u], regs[t]) for u in ts if u < t):
                            break
                        if ready[t] > h:
                            break
                        fn = stages[t][s_]
                        stage[t] = s_ + 1
                        if fn is None:
                            continue
                        cur.append(fn)
                        ready[t] = h + lat[s_]
                        break
                hooks.append(cur)
                h += 1
                assert h < 1000
            return hooks

        def run_hook(hooks):
            if hooks:
                for fn in hooks.pop(0):
                    fn()

        def flush_hooks(hooks):
            while hooks:
                run_hook(hooks)

        def update_pass(*a, **kw):
            stages, regs = update_stages(*a, **kw)
            flush_hooks(sched_tiles(range(NT_U), stages, regs))

        def out_proj_to_ys(wt, l, kcn, rhs_fn, rhs_keys, halves, stage_off, bg=None, bg_hf=1):
            pair = [0]
            for hf in halves:
                for m in range(16):
                    if bg is not None and hf == bg_hf:
                        run_hook(bg)
                    pieces = []
                    for k0 in range(0, kcn, 16):
                        kn = min(16, kcn - k0)
                        pieces.append((k0, kn, load_piece(wt[l * 16 + m][:, k0:k0 + kn, :])))
                    b0 = (pair[0] % 4) * 2
                    pair[0] += 1
                    for (k0, kn, (wv, wk)) in pieces:
                        def mm(e, wv=wv, k0=k0, kn=kn, b0=b0, hf=hf):
                            for kc in range(kn):
                                for tt in range(2):
                                    i = e.matmul(bank(b0 + tt), lhsT=wv[:, kc, :],
                                                 rhs=rhs_fn(k0 + kc, hf * 1024 + tt * 512, 512),
                                                 start=(k0 + kc == 0), stop=(k0 + kc == kcn - 1))
                            return i
                        S.op("pe", mm, reads=wk + rhs_keys(hf), writes=pk(b0, b0 + 1))
                    sp_ = pair[0] % 2
                    YS, ysk = av(stage_off + sp_ * 4 * KB, 4 * KB, F32)
                    S.op("act", lambda e, YS=YS, b0=b0: e.activation(out=YS, in_=PS[:, b0 * 512:(b0 + 2) * 512], func=AF.Copy),
                         reads=pk(b0, b0 + 1), writes=ysk)
                    S.op("sp", lambda e, YS=YS, m=m, hf=hf: e.dma_start(out=ys[4 * hf:4 * hf + 4, :, m * 256:(m + 1) * 256].rearrange("t p n -> p t n"), in_=YS.rearrange("p (t n) -> p t n", t=4)),
                         reads=ysk, writes=[("ys", m, hf)], dma=True)
            if bg is not None:
                flush_hooks(bg)

        def mixer(l, bg_wo=None):
            SCONV, sconv_k = av(BIG_OFF, 32 * KB)
            SCONV = SCONV.rearrange("p (c n) -> p c n", c=8)
            MERG, merg_k = av(BIG_OFF + 32 * KB, 32 * KB)
            MERG = MERG.rearrange("p (c n) -> p c n", c=16)
            AO, ao_k = av(BIG_OFF + 64 * KB, 16 * KB)
            AO = AO.rearrange("p (c n) -> p c n", c=4)
            MASK, mask_k = av(BIG_OFF + 80 * KB, 7 * KB)
            S.op("pool", lambda e: e.dma_start(out=MASK, in_=mask_in[:, 0:28 * 128]), writes=mask_k, dma=True)
            hk_all = ha_keys()

            def zproj(chunk, b0):
                wv, wk = load_piece(w_in_t[l * 84 + chunk])

                def mm(e):
                    for kc in range(16):
                        for tt in range(4):
                            i = e.matmul(bank(b0 + tt), lhsT=wv[:, kc, :], rhs=HA[:, kc, tt * 512:(tt + 1) * 512],
                                         start=(kc == 0), stop=(kc == 15))
                    return i
                S.op("pe", mm, reads=wk + hk_all, writes=pk(*range(b0, b0 + 4)))

            def half_zproj(wv, wk, hf, b0):
                def mm(e):
                    for kc in range(16):
                        for t2 in range(2):
                            i = e.matmul(bank(b0 + t2), lhsT=wv[:, kc, :],
                                         rhs=HA[:, kc, hf * 1024 + t2 * 512: hf * 1024 + (t2 + 1) * 512],
                                         start=(kc == 0), stop=(kc == 15))
                    return i
                S.op("pe", mm, reads=wk + ha_keys(hf * 1024, (hf + 1) * 1024), writes=pk(b0, b0 + 1))

            T0 = BIG_OFF
            QT = [av(T0 + i * 4 * KB, 4 * KB) for i in range(2)]
            KT = [av(T0 + 8 * KB + i * 4 * KB, 4 * KB) for i in range(2)]
            VT = [av(T0 + 16 * KB + i * 4 * KB, 4 * KB) for i in range(2)]
            PB = [av(T0 + 24 * KB + i * KB, KB) for i in range(6)]
            ACCO, acco_k = av(T0 + 32 * KB, 8 * KB, F32)
            ACCD, accd_k = av(T0 + 40 * KB, 8 * KB, F32)
            DIL = ((1, 16), (4, 4), (16, 1))
            heads = [(j, g) for j in range(4) for g in range(3)]

            def proj_items(n):
                j, g = heads[n]
                hd = 4 * g + j
                d, tpr = DIL[g]
                par = n % 2
                qT, qk = QT[par]
                kT, kk = KT[par]
                vT, vk = VT[par]
                state = {}
                qk_items = []
                for (chunk, dstT, dk, nm) in ((16 + hd, qT, qk, "q"), (28 + hd, kT, kk, "k")):
                    for hf in range(2):
                        def it(chunk=chunk, dstT=dstT, dk=dk, nm=nm, hf=hf):
                            if hf == 0:
                                state[nm] = load_piece(w_in_t[l * 84 + chunk])
                            wv, wk = state[nm]
                            half_zproj(wv, wk, hf, 5)
                            for t2 in range(2):
                                tt = 2 * hf + t2
                                n_i = 512 // d
                                dv = dstT.rearrange("p (r i) -> p r i", r=d)[:, :, n_i * tt: n_i * (tt + 1)]
                                sv = bank(5 + t2).rearrange("p (i r) -> p r i", r=d)
                                if t2 == 0:
                                    S.op("act", lambda e, dv=dv, sv=sv: e.activation(out=dv, in_=sv, func=AF.Copy),
                                         reads=pk(5 + t2), writes=dk)
                                else:
                                    S.op("dve", lambda e, dv=dv, sv=sv: e.tensor_copy(out=dv, in_=sv),
                                         reads=pk(5 + t2), writes=dk)
                        qk_items.append(it)
                v_items = []
                for tq in range(4):
                    def itv(tq=tq):
                        if tq == 0:
                            state["v"] = load_piece(w_in_t[l * 84 + 40 + hd])
                        wv_, wvk = state["v"]

                        def vmm(e):
                            for tl in range(4):
                                T = tq * 4 + tl
                                r = T // tpr
                                start = 128 * (T % tpr) * d + r
                                for kc in range(16):
                                    i = e.matmul(bank(7, 128, tl * 128),
                                                 lhsT=HA[:, kc, start: start + 127 * d + 1: d],
                                                 rhs=wv_[:, kc, :], start=(kc == 0), stop=(kc == 15))
                            return i
                        S.op("pe", vmm, reads=wvk + hk_all, writes=pk(7))
                        dstv = vT[:, tq * 512:(tq + 1) * 512]
                        if tq % 2 == 0:
                            S.op("dve", lambda e: e.tensor_copy(out=dstv, in_=bank(7)), reads=pk(7), writes=vk)
                        else:
                            S.op("act", lambda e: e.activation(out=dstv, in_=bank(7), func=AF.Copy), reads=pk(7), writes=vk)
                    v_items.append(itv)
                items = []
                for i in range(4):
                    items.append(qk_items[i])
                    items.append(v_items[i])
                return items

            pset = [0]

            def attn_items(n):
                j, g = heads[n]
                hd = 4 * g + j
                d, tpr = DIL[g]
                par = n % 2
                qT, qk = QT[par]
                kT, kk = KT[par]
                vT, vk = VT[par]
                vT3 = vT.rearrange("p (t c) -> p t c", t=16)
                offs = (0, -1, 1) if tpr > 1 else (0,)
                out = []
                for c in range(4):
                    st_ = {}

                    def s_item(c=c, st_=st_):
                        pb0 = 0 if pset[0] % 2 == 0 else 3
                        pset[0] += 1
                        rng_o = {}
                        for oi, o in enumerate(offs):
                            tls = [tl for tl in range(4) if 0 <= ((4 * c + tl) % tpr) + o < tpr]
                            if not tls:
                                continue
                            lo, hi = tls[0], tls[-1] + 1
                            rng_o[o] = (lo, hi, oi)
                            sbk = oi
                            P, pkk = PB[pb0 + oi]

                            def smm(e, o=o, lo=lo, hi=hi, sbk=sbk):
                                for tl in range(lo, hi):
                                    T = 4 * c + tl
                                    i = e.matmul(bank(sbk, 128, tl * 128), lhsT=kT[:, 128 * (T + o):128 * (T + o + 1)],
                                                 rhs=qT[:, 128 * T:128 * (T + 1)], start=True, stop=True)
                                return i
                            S.op("pe", smm, reads=qk + kk, writes=pk(sbk))
                            S.op("act", lambda e, P=P, sbk=sbk, lo=lo, hi=hi: e.activation(
                                out=P[:, lo * 128:hi * 128], in_=bank(sbk, (hi - lo) * 128, lo * 128),
                                func=AF.Exp, scale=float(128 ** -0.5)), reads=pk(sbk), writes=pkk)
                            mi = mask_index(hd, o)
                            mv = MASK[:, mi * 128:(mi + 1) * 128].unsqueeze(1).broadcast_to([128, hi - lo, 128])
                            S.op("dve", lambda e, P=P, lo=lo, hi=hi, mv=mv: e.tensor_tensor(
                                out=P[:, lo * 128:hi * 128].rearrange("p (t n) -> p t n", n=128),
                                in0=P[:, lo * 128:hi * 128].rearrange("p (t n) -> p t n", n=128),
                                in1=mv, op=ALU.mult), reads=pkk + mask_k, writes=pkk)
                        st_["rng_o"] = rng_o
                        st_["pb0"] = pb0

                    def pv_item(c=c, st_=st_):
                        rng_o = st_["rng_o"]
                        pb0 = st_["pb0"]
                        all_pk = []
                        for o in rng_o:
                            all_pk += PB[pb0 + rng_o[o][2]][1]

                        def pvmm(e):
                            for tl in range(4):
                                T = 4 * c + tl
                                vo = [o for o in rng_o if rng_o[o][0] <= tl < rng_o[o][1]]
                                for idx, o in enumerate(vo):
                                    P = PB[pb0 + rng_o[o][2]][0]
                                    i = e.matmul(bank(3, 128, tl * 128), lhsT=vT3[:, T + o, :],
                                                 rhs=P[:, tl * 128:(tl + 1) * 128],
                                                 start=(idx == 0), stop=(idx == len(vo) - 1))
                            os_ = list(rng_o.keys())
                            for idx, o in enumerate(os_):
                                lo, hi, oi = rng_o[o]
                                P = PB[pb0 + oi][0]
                                i = e.matmul(bank(4, (hi - lo) * 128, lo * 128), lhsT=ONES[:],
                                             rhs=P[:, lo * 128:hi * 128], start=(idx == 0), stop=(idx == len(os_) - 1))
                            return i
                        S.op("pe", pvmm, reads=all_pk + vk + ["ones"], writes=pk(3, 4))
                        if g == 0:
                            ov = ACCO[:, 512 * c:512 * (c + 1)]
                            dvw = ACCD[:, 512 * c:512 * (c + 1)]
                            S.op("act", lambda e: e.activation(out=ov, in_=bank(3), func=AF.Copy),
                                 reads=pk(3), writes=acco_k)
                            S.op("dve", lambda e: e.tensor_copy(out=dvw, in_=bank(4)),
                                 reads=pk(4), writes=accd_k)
                        else:
                            if g == 1:
                                ov = ACCO.rearrange("p (i r) -> p r i", r=4)[:, c, :]
                                dvw = ACCD.rearrange("p (i r) -> p r i", r=4)[:, c, :]
                                so = bank(3)
                                sd = bank(4)
                            else:
                                ov = ACCO.rearrange("p (i r) -> p r i", r=16)[:, 4 * c:4 * c + 4, :]
                                dvw = ACCD.rearrange("p (i r) -> p r i", r=16)[:, 4 * c:4 * c + 4, :]
                                so = bank(3).rearrange("p (t i) -> p t i", t=4)
                                sd = bank(4).rearrange("p (t i) -> p t i", t=4)
                            S.op("dve", lambda e: e.tensor_tensor(out=ov, in0=ov, in1=so, op=ALU.add),
                                 reads=pk(3) + acco_k, writes=acco_k)
                            S.op("dve", lambda e: e.tensor_tensor(out=dvw, in0=dvw, in1=sd, op=ALU.add),
                                 reads=pk(4) + accd_k, writes=accd_k)
                        if g == 2 and c == 3:
                            S.op("dve", lambda e: e.reciprocal(out=ACCD, in_=ACCD), reads=accd_k, writes=accd_k)
                            S.op("dve", lambda e: e.tensor_tensor(out=AO[:, j, :], in0=ACCO, in1=ACCD, op=ALU.mult),
                                 reads=acco_k + accd_k, writes=ao_k)
                    out.append((s_item, pv_item))
                return out

            for it in proj_items(0):
                it()
            for n in range(12):
                A = attn_items(n)
                P = proj_items(n + 1) if n + 1 < 12 else []
                for c in range(4):
                    if P:
                        P[2 * c]()
                    A[c][0]()
                    if P:
                        P[2 * c + 1]()
                    A[c][1]()

            if l == 0:
                dump("AO", AO, ao_k)
                dump("QT", QT[0][0], QT[0][1])
                dump("KT", KT[0][0], KT[0][1])
                dump("VT", VT[0][0], VT[0][1])
            T1 = BIG_OFF + 32 * KB
            SIG = [av(T1 + i * 4 * KB, 4 * KB) for i in range(2)]
            UPAD = [av(T1 + 8 * KB + i * 4352, 4352) for i in range(2)]
            DG = [av(T1 + 8 * KB + 2 * 4352 + i * 7936, 7936) for i in range(2)]
            for i in range(2):
                u, uk = UPAD[i]
                S.op("pool", lambda e, u=u: e.memset(u[:, 0:2176], 0.0), writes=uk)
            pair_rr = [0]

            def next_pair():
                b = (pair_rr[0] % 4) * 2
                pair_rr[0] += 1
                return b

            def conv_cc(cc):
                par = cc % 2
                u, uk = UPAD[par]
                dg, dgk = DG[par]
                dg3 = dg.rearrange("p (k c) -> p k c", k=31)
                for hf in range(2):
                    b0 = next_pair()

                    def cmm(e, hf=hf, b0=b0):
                        for k in range(31):
                            for t2 in range(2):
                                tt = 2 * hf + t2
                                i = e.matmul(bank(b0 + t2), lhsT=dg3[:, k, :], rhs=u[:, tt * 512 + k: tt * 512 + k + 512],
                                             start=(k == 0), stop=(k == 30))
                        return i
                    S.op("pe", cmm, reads=dgk + uk, writes=pk(b0, b0 + 1))
                    S.op("act", lambda e, hf=hf, b0=b0: e.activation(out=SCONV[:, cc, hf * 1024:(hf + 1) * 1024],
                                                                      in_=PS[:, b0 * 512:(b0 + 2) * 512],
                                                                      func=AF.Identity, bias=cvec(l, 0, cc)),
                         reads=pk(b0, b0 + 1) + ["vec"], writes=sconv_k)

            for cc in range(8):
                par = cc % 2
                sg, sgk = SIG[par]
                u, uk = UPAD[par]
                dg, dgk = DG[par]
                dg3 = dg.rearrange("p (k c) -> p k c", k=31)
                S.op("dve", lambda e, dg3=dg3, cc=cc: e.tensor_tensor(
                    out=dg3, in0=IDENT[:].unsqueeze(1).broadcast_to([128, 31, 128]),
                    in1=wdw(l, cc).unsqueeze(2).broadcast_to([128, 31, 128]), op=ALU.mult),
                    reads=["ident", "vec"], writes=dgk)
                gw, gwk = load_piece(w_in_t[l * 84 + 8 + cc])
                for hf in range(2):
                    b0 = next_pair()
                    half_zproj(gw, gwk, hf, b0)
                    S.op("act", lambda e, sg=sg, hf=hf, b0=b0: e.activation(out=sg[:, hf * 1024:(hf + 1) * 1024],
                                                                             in_=PS[:, b0 * 512:(b0 + 2) * 512], func=AF.Sigmoid),
                         reads=pk(b0, b0 + 1), writes=sgk)
                aw, awk = load_piece(w_in_t[l * 84 + cc])
                for hf in range(2):
                    b0 = next_pair()
                    half_zproj(aw, awk, hf, b0)
                    S.op("dve", lambda e, u=u, sg=sg, hf=hf, b0=b0: e.tensor_tensor(
                        out=u[:, 15 + hf * 1024: 15 + (hf + 1) * 1024], in0=PS[:, b0 * 512:(b0 + 2) * 512],
                        in1=sg[:, hf * 1024:(hf + 1) * 1024], op=ALU.mult), reads=pk(b0, b0 + 1) + sgk, writes=uk)
                if cc > 0:
                    conv_cc(cc - 1)
            conv_cc(7)
            SQC, sqck = av(SCR_OFF, 8 * KB)
            SQC = SQC.rearrange("p (c n) -> p c n", c=8)
            MEAN, mk_ = av(SCR_OFF + 8 * KB, 2 * KB, F32)
            RSTD, rk_ = av(SCR_OFF + 10 * KB, 2 * KB, F32)
            M2, m2k = av(SCR_OFF + 12 * KB, 2 * KB, F32)
            TMP = [av(SCR_OFF + 14 * KB + i * 2 * KB, 2 * KB, F32) for i in range(2)]
            for tt in range(4):
                tsl = slice(tt * 512, (tt + 1) * 512)
                S.op("act", lambda e, tsl=tsl: e.activation(out=SQC, in_=SCONV[:, :, tsl], func=AF.Square),
                     reads=sconv_k, writes=sqck)
                b1 = next_bank()
                b2 = next_bank()

                def st1(e, b1=b1, b2=b2, tsl=tsl):
                    for cc in range(8):
                        i = e.matmul(bank(b1), lhsT=ONES[:], rhs=SCONV[:, cc, tsl], start=(cc == 0), stop=(cc == 7))
                    for cc in range(8):
                        i = e.matmul(bank(b2), lhsT=ONES[:], rhs=SQC[:, cc, :], start=(cc == 0), stop=(cc == 7))
                    return i
                S.op("pe", st1, reads=sconv_k + sqck + ["ones"], writes=pk(b1, b2))
                S.op("act", lambda e, b1=b1: e.activation(out=MEAN, in_=bank(b1), func=AF.Copy, scale=1.0 / 1024),
                     reads=pk(b1), writes=mk_)
                S.op("dve", lambda e: e.tensor_tensor(out=M2, in0=MEAN, in1=MEAN, op=ALU.mult), reads=mk_, writes=m2k)
                S.op("dve", lambda e, b2=b2: e.scalar_tensor_tensor(out=RSTD, in0=bank(b2), scalar=1.0 / 1024, in1=M2,
                                                                    op0=ALU.mult, op1=ALU.subtract),
                     reads=pk(b2) + m2k, writes=rk_)
                S.op("act", lambda e: e.activation(out=RSTD, in_=RSTD, func=AF.Sqrt, bias=EPST[:]),
                     reads=rk_ + ["eps"], writes=rk_)
                S.op("dve", lambda e: e.reciprocal(out=RSTD, in_=RSTD), reads=rk_, writes=rk_)
                for cc in range(8):
                    tmp, tk = TMP[cc % 2]
                    S.op("pool", lambda e, tmp=tmp, cc=cc, tsl=tsl: e.tensor_tensor(out=tmp, in0=SCONV[:, cc, tsl], in1=MEAN,
                                                                                     op=ALU.subtract),
                         reads=sconv_k + mk_, writes=tk)
                    S.op("dve", lambda e, tmp=tmp: e.tensor_tensor(out=tmp, in0=tmp, in1=RSTD, op=ALU.mult),
                         reads=tk + rk_, writes=tk)
                    S.op("act", lambda e, tmp=tmp, cc=cc, tsl=tsl: e.activation(out=SCONV[:, cc, tsl], in_=tmp, func=AF.Silu,
                                                                                 scale=cvec(l, 1, cc), bias=cvec(l, 2, cc)),
                         reads=tk + ["vec"], writes=sconv_k)

            if l == 0:
                dump("SCONV", SCONV, sconv_k)
            SGA, sgak = av(SCR_OFF, 4 * KB, F32)
            SGB, sgbk = av(SCR_OFF + 4 * KB, 4 * KB, F32)
            TM, tmk = av(SCR_OFF + 8 * KB, 4 * KB, F32)
            T2, t2k = av(SCR_OFF + 12 * KB, 4 * KB, F32)
            pair = 0
            for hf in range(2):
                hsl = slice(hf * 1024, (hf + 1) * 1024)
                hk_h = ha_keys(hf * 1024, (hf + 1) * 1024)

                def half_proj(src, kcn, rhs3, rkeys, b0, hf=hf):
                    wv, wk = load_piece(src)

                    def mm(e):
                        for kc in range(kcn):
                            for tt in range(2):
                                i = e.matmul(bank(b0 + tt), lhsT=wv[:, kc, :],
                                             rhs=rhs3[:, kc, hf * 1024 + tt * 512: hf * 1024 + (tt + 1) * 512],
                                             start=(kc == 0), stop=(kc == kcn - 1))
                        return i
                    S.op("pe", mm, reads=wk + rkeys, writes=pk(b0, b0 + 1))

                for m in range(16):
                    half_proj(w_in_t[l * 84 + 52 + m], 16, HA, hk_h, 0)
                    S.op("act", lambda e: e.activation(out=SGA, in_=PS[:, 0:1024], func=AF.Sigmoid),
                         reads=pk(0, 1), writes=sgak)
                    half_proj(w_co_t[l * 16 + m], 8, SCONV, sconv_k, 2)
                    S.op("dve", lambda e: e.tensor_tensor(out=TM, in0=PS[:, 1024:2048], in1=SGA, op=ALU.mult),
                         reads=pk(2, 3) + sgak, writes=tmk)
                    half_proj(w_in_t[l * 84 + 68 + m], 16, HA, hk_h, 4)
                    S.op("act", lambda e: e.activation(out=SGB, in_=PS[:, 2048:3072], func=AF.Sigmoid),
                         reads=pk(4, 5), writes=sgbk)
                    half_proj(w_ap_t[l * 16 + m], 4, AO, ao_k, 6)
                    S.op("dve", lambda e: e.tensor_tensor(out=T2, in0=PS[:, 3072:4096], in1=SGB, op=ALU.mult),
                         reads=pk(6, 7) + sgbk, writes=t2k)
                    S.op("dve", lambda e, m=m: e.tensor_tensor(out=MERG[:, m, :], in0=TM, in1=T2, op=ALU.add),
                         reads=tmk + t2k, writes=merg_k)
                out_proj_to_ys(w_o_t, l, 16, lambda kc, t0, n, hf=hf: MERG[:, kc, t0 - hf * 1024: t0 - hf * 1024 + n],
                               lambda hf_: merg_k, [hf], SCR_OFF + 16 * KB, bg=(bg_wo if hf == 1 else None), bg_hf=1)

        def cross(l, bg=None, bg_co=None):
            QX, qxk = av(BIG_OFF + 16 * KB, 64 * KB)
            QX = QX.rearrange("p (c n) -> p c n", c=16)
            KX, kxk = av(BIG_OFF, 8 * KB)
            KX = KX.rearrange("p (c n) -> p c n", c=16)
            VX, vxk = av(BIG_OFF + 8 * KB, 8 * KB)
            VX = VX.rearrange("p (t f) -> p t f", t=2)
            MEMN, mnk = av(BIG_OFF + 80 * KB, 8 * KB)
            MEMN = MEMN.rearrange("p (c n) -> p c n", c=16)
            MEMF, mfk = av(SCR_OFF, 16 * KB, F32)
            MEMF = MEMF.rearrange("p (c n) -> p c n", c=16)
            MSQ, msk = av(SCR_OFF + 16 * KB, 8 * KB)
            MSQ = MSQ.rearrange("p (c n) -> p c n", c=16)
            RM, rmk = av(SCR_OFF + 24 * KB, KB, F32)
            hk_all = ha_keys()
            S.op("sp", lambda e: e.dma_start(out=MEMF, in_=memT_in.rearrange("(c p) n -> p c n", p=128)), writes=mfk, dma=True)
            S.op("act", lambda e: e.activation(out=MSQ, in_=MEMF, func=AF.Square), reads=mfk, writes=msk)
            b = next_bank()

            def ssq(e, b=b):
                for c in range(16):
                    i = e.matmul(bank(b, 256), lhsT=ONES[:], rhs=MSQ[:, c, :], start=(c == 0), stop=(c == 15))
                return i
            S.op("pe", ssq, reads=msk + ["ones"], writes=pk(b))
            rstd_from_bank(b, 256, RM, rmk, 1.0 / D)

            def mkm(e):
                for c in range(16):
                    i = e.scalar_tensor_tensor(out=MEMN[:, c, :], in0=MEMF[:, c, :], scalar=vec(l, G_MEM, c), in1=RM,
                                               op0=ALU.mult, op1=ALU.mult)
                return i
            S.op("dve", mkm, reads=mfk + rmk + ["vec"], writes=mnk)
            for m in range(16):
                run_hook(bg)
                wv, wk = load_piece(w_ck_t[l * 16 + m])
                b = next_bank()

                def kmm(e, wv=wv, b=b):
                    for kc in range(16):
                        i = e.matmul(bank(b, 256), lhsT=wv[:, kc, :], rhs=MEMN[:, kc, :], start=(kc == 0), stop=(kc == 15))
                    return i
                S.op("pe", kmm, reads=wk + mnk, writes=pk(b))
                S.op("act", lambda e, m=m, b=b: e.activation(out=KX[:, m, :], in_=bank(b, 256), func=AF.Copy),
                     reads=pk(b), writes=kxk)
            for m in range(16):
                run_hook(bg)
                wv, wk = load_piece(w_cv_t[l * 16 + m])
                b = next_bank()

                def vmm(e, wv=wv, b=b):
                    for mt in range(2):
                        for kc in range(16):
                            i = e.matmul(bank(b, 128, mt * 128), lhsT=MEMN[:, kc, mt * 128:(mt + 1) * 128], rhs=wv[:, kc, :],
                                         start=(kc == 0), stop=(kc == 15))
                    return i
                S.op("pe", vmm, reads=wk + mnk, writes=pk(b))
                S.op("dve", lambda e, m=m, b=b: e.tensor_copy(out=VX[:, :, m * 128:(m + 1) * 128],
                                                              in_=bank(b, 256).rearrange("p (t f) -> p t f", t=2)),
                     reads=pk(b), writes=vxk)
            flush_hooks(bg)
            for m in range(16):
                wv, wk = load_piece(w_cq_t[l * 16 + m])
                b0 = (m % 2) * 4

                def qmm(e, wv=wv, b0=b0):
                    for kc in range(16):
                        for tt in range(4):
                            i = e.matmul(bank(b0 + tt), lhsT=wv[:, kc, :], rhs=HA[:, kc, tt * 512:(tt + 1) * 512],
                                         start=(kc == 0), stop=(kc == 15))
                    return i
                S.op("pe", qmm, reads=wk + hk_all, writes=pk(*range(b0, b0 + 4)))
                for tt in range(4):
                    if tt % 2 == 0:
                        S.op("act", lambda e, m=m, tt=tt, b0=b0: e.activation(out=QX[:, m, tt * 512:(tt + 1) * 512], in_=bank(b0 + tt),
                                                                               func=AF.Copy), reads=pk(b0 + tt), writes=qxk)
                    else:
                        S.op("dve", lambda e, m=m, tt=tt, b0=b0: e.tensor_copy(out=QX[:, m, tt * 512:(tt + 1) * 512], in_=bank(b0 + tt)),
                             reads=pk(b0 + tt), writes=qxk)
            PX = [av(SCR_OFF + i * KB, KB) for i in range(4)]
            RD, rdk = av(SCR_OFF + 4 * KB, 2 * KB, F32)
            iters = [(hx, qt) for hx in range(4) for qt in range(4)]

            def x_scores(i):
                hx, qt = iters[i]
                qsl = slice(qt * 512, (qt + 1) * 512)
                sb = (i % 2) * 2

                def smm(e):
                    for mt in range(2):
                        for dc in range(4):
                            ii = e.matmul(bank(sb + mt), lhsT=KX[:, 4 * hx + dc, mt * 128:(mt + 1) * 128], rhs=QX[:, 4 * hx + dc, qsl],
                                          start=(dc == 0), stop=(dc == 3))
                    return ii
                S.op("pe", smm, reads=kxk + qxk, writes=pk(sb, sb + 1))

            def x_rest(i):
                hx, qt = iters[i]
                qsl = slice(qt * 512, (qt + 1) * 512)
                sb = (i % 2) * 2
                pp = (i % 2) * 2
                for mt in range(2):
                    P, pkk = PX[pp + mt]
                    S.op("act", lambda e, P=P, mt=mt: e.activation(out=P, in_=bank(sb + mt), func=AF.Exp, scale=float(512 ** -0.5)),
                         reads=pk(sb + mt), writes=pkk)
                p0, p0k = PX[pp]
                p1, p1k = PX[pp + 1]

                def pv(e):
                    for mt, P in ((0, p0), (1, p1)):
                        ii = e.matmul(bank(sb), lhsT=ONES[:], rhs=P, start=(mt == 0), stop=(mt == 1))
                    for dc in range(4):
                        for mt, P in ((0, p0), (1, p1)):
                            ii = e.matmul(bank(4 + dc), lhsT=VX[:, mt, (4 * hx + dc) * 128:(4 * hx + dc + 1) * 128], rhs=P,
                                          start=(mt == 0), stop=(mt == 1))
                    return ii
                S.op("pe", pv, reads=p0k + p1k + vxk + ["ones"], writes=pk(sb, 4, 5, 6, 7))
                S.op("dve", lambda e: e.reciprocal(out=RD, in_=bank(sb)), reads=pk(sb), writes=rdk)
                for dc in range(4):
                    S.op("dve", lambda e, dc=dc: e.tensor_tensor(out=HA[:, 4 * hx + dc, qsl], in0=bank(4 + dc), in1=RD, op=ALU.mult),
                         reads=pk(4 + dc) + rdk, writes=ha_keys(qsl.start, qsl.stop))

            x_scores(0)
            for i in range(16):
                if i + 1 < 16:
                    x_scores(i + 1)
                x_rest(i)
            out_proj_to_ys(w_cx_t, l, 16, lambda kc, t0, n: HA[:, kc, t0:t0 + n],
                           lambda hf_: ha_keys(hf_ * 1024, (hf_ + 1) * 1024), [0, 1], SCR_OFF + 8 * KB, bg=bg_co, bg_hf=1)

        def ffn(l, bg0=None, bg1=None):
            ACTH, ahk = av(BIG_OFF, 88 * KB)
            ACTH = ACTH.rearrange("p (c n) -> p c n", c=44)
            SG = [av(SCR_OFF + i * 4 * KB, 4 * KB, F32) for i in range(2)]
            for hf in range(2):
                hk_h = ha_keys(hf * 1024, (hf + 1) * 1024)
                for f in range(44):
                    sg, sgk = SG[f % 2]
                    b0 = (f % 2) * 4
                    run_hook(bg0 if hf == 0 else bg1)
                    for (wt, bb) in ((w_fg_t, b0), (w_fu_t, b0 + 2)):
                        wv, wk = load_piece(wt[l * 44 + f])

                        def mm(e, wv=wv, bb=bb, hf=hf):
                            for kc in range(16):
                                for tt in range(2):
                                    i = e.matmul(bank(bb + tt), lhsT=wv[:, kc, :],
                                                 rhs=HA[:, kc, hf * 1024 + tt * 512: hf * 1024 + (tt + 1) * 512],
                                                 start=(kc == 0), stop=(kc == 15))
                            return i
                        S.op("pe", mm, reads=wk + hk_h, writes=pk(bb, bb + 1))
                    S.op("act", lambda e, sg=sg, b0=b0: e.activation(out=sg, in_=PS[:, b0 * 512:(b0 + 2) * 512], func=AF.Silu),
                         reads=pk(b0, b0 + 1), writes=sgk)
                    fk = [("A", k) for k in range((BIG_OFF + f * 2048) // KB, (BIG_OFF + f * 2048 + 2047) // KB + 1)]
                    S.op("dve", lambda e, sg=sg, b0=b0, f=f: e.tensor_tensor(out=ACTH[:, f, :], in0=PS[:, (b0 + 2) * 512:(b0 + 4) * 512], in1=sg,
                                                                              op=ALU.mult),
                         reads=pk(b0 + 2, b0 + 3) + sgk, writes=fk)
                if (bg0 if hf == 0 else bg1) is not None:
                    flush_hooks(bg0 if hf == 0 else bg1)
                out_proj_to_ys(w_fd_t, l, 44, lambda kc, t0, n, hf=hf: ACTH[:, kc, t0 - hf * 1024: t0 - hf * 1024 + n],
                               lambda hf_: ahk, [hf], SCR_OFF + 8 * KB)

        update_pass(0, xT_in, False, None, None, G_MIX_PRE)
        dump("HA0", HA, ha_keys())
        x_cur = xT_in
        stop = dbg
        done = False
        for l in range(L):
            if l > 0:
                S.new_epoch()
            if stop == (l, "mixer"):
                mixer(l)
                update_pass(l, x_cur, True, G_MIX_POST, outT, None, final=True)
                done = True
                break
            last = (l == L - 1)
            A1 = (BIG_OFF, BIG_OFF + 16 * KB, BIG_OFF + 64 * KB)
            U1B = (BIG_OFF + 32 * KB, BIG_OFF + 48 * KB, BIG_OFF + 72 * KB)
            st1, rg1 = update_stages(l, x_cur, True, G_MIX_POST, xs, G_X_PRE, regs=[A1] * 4 + [U1B] * 4, add_eng="dve")
            h1a = sched_tiles(range(0, 4), st1, rg1, lat=(2, 1, 2, 1))
            h1b = sched_tiles(range(4, 8), st1, rg1, lat=(4, 2, 4, 2))
            mixer(l, bg_wo=h1a)
            x_cur = xs
            if stop == (l, "cross"):
                cross(l, bg=h1b)
                update_pass(l, xs, True, G_X_POST, outT, None, final=True)
                done = True
                break
            st2, rg2 = update_stages(l, xs, True, G_X_POST, xs, G_FFN_PRE, regs=[UP, LO, UP, LO] + [TOP] * 4, add_eng="dve")
            h2a = sched_tiles(range(0, 4), st2, rg2, lat=(2, 1, 2, 1))
            h2b = sched_tiles(range(4, 8), st2, rg2, lat=(2, 1, 1, 1))
            assert len(h2b) <= 23, len(h2b)
            cross(l, bg=h1b, bg_co=h2a)
            st3, rg3 = update_stages(l, xs, True, G_FFN_POST, (outT if last else xs), (None if last else G_MIX_PRE),
                                     final=last, l_pre=l + 1, regs=[TOP] * 4 + [UP, LO, UP, LO],
                                     add_eng=(lambda t: "dve" if t < 4 else "pool"))
            h3a = sched_tiles(range(0, 4), st3, rg3, lat=(2, 1, 1, 1))
            assert len(h3a) <= 23, len(h3a)
            u3_bg = (OVL_U3 & 2) if last else (OVL_U3 & 1)
            if u3_bg:
                ffn(l, bg0=h2b, bg1=h3a)
            else:
                ffn(l, bg0=h2b)
                flush_hooks(h3a)
            flush_hooks(sched_tiles(range(4, 8), st3, rg3))
        S.op("sp", lambda e: e.nop(), reads=[("out", t) for t in range(NT_U)])
        S.emit(nc, st)
    return nc


def _tile_w(w, kc):
    Lw, K, N = w.shape
    t = w.reshape(Lw, K // 128, 128, N // 128, 128).transpose(0, 3, 2, 1, 4)
    return np.ascontiguousarray(t).reshape(Lw * (N // 128), 128, K // 128, 128)


def _alibi_masks():
    slopes = np.exp2(-8.0 * (np.arange(12, dtype=np.float64) + 1.0) / 12)
    kk = np.arange(128)[:, None]
    qq = np.arange(128)[None, :]
    out = np.zeros((128, 29 * 128), np.float32)
    out[:, 28 * 128:] = np.eye(128, dtype=np.float32)
    dil = (1, 4, 16)
    for hd in range(12):
        g = hd // 4
        for o in ((-1, 0, 1) if g < 2 else (0,)):
            rel = 128 * o + kk - qq
            m = np.where(np.abs(rel) <= 64, np.exp(-slopes[hd] * dil[g] * np.abs(rel)), 0.0)
            mi = mask_index(hd, o)
            out[:, mi * 128:(mi + 1) * 128] = m.astype(np.float32)
    return out


def _pack_vecs(inp, L):
    v = np.zeros((128, L * NVEC), np.float32)
    names = ["g_mix_pre", "g_mix_post", "g_mem", "g_x_pre", "g_x_post", "g_ffn_pre", "g_ffn_post"]
    for l in range(L):
        base = l * NVEC
        for i, n in enumerate(names):
            v[:, base + i * 16: base + (i + 1) * 16] = np.asarray(inp[n][l], np.float32).reshape(16, 128).T
        for i, n in enumerate(["b_dw", "conv_ln_g", "conv_ln_b"]):
            v[:, base + 112 + i * 8: base + 112 + (i + 1) * 8] = np.asarray(inp[n][l], np.float32).reshape(8, 128).T
        wd = np.asarray(inp["w_dw"][l], np.float32)
        v[:, base + 136: base + 136 + 248] = wd.T.reshape(8, 128, 31).transpose(1, 0, 2).reshape(128, 248)
    return v


_NC_CACHE = {}
_DBG = None
OVL_WO = True
OVL_CO = True
OVL_U3 = 3
_DUMP = False
_DUMP_RES = {}


def kernel(**inputs):
    L = DEPTH
    f = lambda k: np.asarray(inputs[k], np.float32)
    shared = {
        "w_in": _tile_w(f("w_in"), 16),
        "w_conv_out": _tile_w(f("w_conv_out"), 8),
        "w_attn_proj": _tile_w(f("w_attn_proj"), 4),
        "w_o": _tile_w(f("w_o"), 16),
        "w_cq": _tile_w(f("w_cq"), 16),
        "w_ck": _tile_w(f("w_ck"), 16),
        "w_cv": _tile_w(f("w_cv"), 16),
        "w_co": _tile_w(f("w_co"), 16),
        "w_ffn_gate": _tile_w(f("w_ffn_gate"), 16),
        "w_ffn_up": _tile_w(f("w_ffn_up"), 16),
        "w_ffn_down": _tile_w(f("w_ffn_down"), 44),
        "vecs": _pack_vecs(inputs, L),
        "masks": _alibi_masks(),
    }
    x = f("x")
    mem = f("mem")
    in_maps = []
    for b in range(8):
        m = dict(shared)
        m["xT"] = np.ascontiguousarray(x[b].reshape(8, 256, 16, 128).transpose(0, 3, 2, 1)).reshape(8, 128, 4096)
        m["memT"] = np.ascontiguousarray(mem[b].T)
        in_maps.append(m)
    if "nc" not in _NC_CACHE:
        _NC_CACHE["nc"] = build_program(L, _DBG)
    res = run_bass_kernel_spmd(_NC_CACHE["nc"], in_maps, core_ids=list(range(8)))
    if _DUMP:
        for k_ in res.results[0]:
            if k_.startswith("dump_"):
                _DUMP_RES[k_] = np.asarray(res.results[0][k_])
    out = np.stack([np.ascontiguousarray(np.asarray(res.results[b]["outT"]).reshape(8, 128, 16, 256).transpose(0, 3, 2, 1)).reshape(SEQ, D)
                    for b in range(8)], axis=0)
    return out.astype(np.float32)
```

```python
import contextlib
import numpy as np
import concourse.bass as bass
import concourse.mybir as mybir
from concourse.bass_utils import run_bass_kernel_spmd

F32 = mybir.dt.float32
BF16 = mybir.dt.bfloat16
AF = mybir.ActivationFunctionType
ALU = mybir.AluOpType

D = 2048
SEQ = 2048
NMEM = 256
DEPTH = 4
DFF = 5632
NCH_IN = 84
EPS = 1e-6
NVEC = 384

ENGS = ("pe", "act", "dve", "pool", "sp")
N_DMA_SEMS = 12


class Op:
    __slots__ = ("eng", "fn", "deps", "signal", "val", "is_dma", "sem", "epoch", "extra_wait")

    def __init__(self, eng, fn, is_dma, epoch):
        self.eng = eng
        self.fn = fn
        self.deps = []
        self.signal = is_dma
        self.val = None
        self.is_dma = is_dma
        self.sem = None
        self.epoch = epoch
        self.extra_wait = None


class Sched:
    def __init__(self):
        self.ops = {e: [] for e in ENGS}
        self.last_w = {}
        self.readers = {}
        self.epoch = 0

    def new_epoch(self):
        self.epoch += 1

    def op(self, eng, fn, reads=(), writes=(), dma=False):
        o = Op(eng, fn, dma, self.epoch)
        deps = {}
        for r in reads:
            w = self.last_w.get(r)
            if w is not None:
                deps[id(w)] = w
        for r in writes:
            w = self.last_w.get(r)
            if w is not None:
                deps[id(w)] = w
            for rd in self.readers.get(r, ()):
                deps[id(rd)] = rd
        for d in deps.values():
            if d is o:
                continue
            if d.eng == "pe" and eng == "pe" and not d.is_dma and not dma:
                continue
            d.signal = True
            o.deps.append(d)
        for r in writes:
            self.last_w[r] = o
            self.readers[r] = []
        for r in reads:
            self.readers.setdefault(r, []).append(o)
        self.ops[eng].append(o)
        return o

    def emit(self, nc, stack):
        nep = self.epoch + 1
        esem = {}
        for e in ENGS:
            for ep in range(nep):
                esem[(e, ep)] = stack.enter_context(nc.semaphore(f"s_{e}_{ep}"))
        dsem = {}
        for e in ("sp", "pool", "act"):
            dsem[e] = [stack.enter_context(nc.semaphore(f"d_{e}_{i}")) for i in range(N_DMA_SEMS)]
        for e in ENGS:
            cnt = {}
            k = 0
            for o in self.ops[e]:
                if o.is_dma:
                    o.sem = dsem[e][k % N_DMA_SEMS]
                    o.val = 16 * (k // N_DMA_SEMS + 1)
                    if k >= N_DMA_SEMS:
                        o.extra_wait = (o.sem, 16 * (k // N_DMA_SEMS))
                    k += 1
                elif o.signal:
                    c = cnt.get(o.epoch, 0) + 1
                    cnt[o.epoch] = c
                    o.val = c
                    o.sem = esem[(e, o.epoch)]
        block = stack.enter_context(nc.Block())

        def replay(e):
            def body(eng):
                seen = {}
                for o in self.ops[e]:
                    waits = []
                    if o.extra_wait is not None:
                        waits.append(o.extra_wait)
                    for d in o.deps:
                        waits.append((d.sem, d.val))
                    for (s, v) in waits:
                        key = id(s)
                        if seen.get(key, 0) >= v:
                            continue
                        seen[key] = v
                        eng.wait_ge(s, v)
                    inst = o.fn(eng)
                    if o.signal:
                        inst.then_inc(o.sem, 16 if o.is_dma else 1)
            return body

        block.tensor(replay("pe"))
        block.scalar(replay("act"))
        block.vector(replay("dve"))
        block.gpsimd(replay("pool"))
        block.sync(replay("sp"))


KB = 1024
HA_OFF = 0
BIG_OFF = 64 * KB
RING_OFF = BIG_OFF + 88 * KB
NSLOT = 4
SCR_OFF = RING_OFF + NSLOT * 4 * KB
SCR_SIZE = 28 * KB
ARENA_BYTES = SCR_OFF + SCR_SIZE


def mask_index(hd, o):
    g = hd // 4
    if g < 2:
        return hd * 3 + (o + 1)
    return 24 + (hd - 8)


def build_program(depth=DEPTH, dbg=None):
    nc = bass.Bass("TRN2", target_bir_lowering=False)
    L = depth

    def dram(name, shape, kind):
        return nc.dram_tensor(name, list(shape), F32, kind=kind).ap()

    xT_in = dram("xT", [8, 128, 16 * 256], "ExternalInput")
    memT_in = dram("memT", [D, NMEM], "ExternalInput")
    w_in_t = dram("w_in", [L * 84, 128, 16, 128], "ExternalInput")
    w_co_t = dram("w_conv_out", [L * 16, 128, 8, 128], "ExternalInput")
    w_ap_t = dram("w_attn_proj", [L * 16, 128, 4, 128], "ExternalInput")
    w_o_t = dram("w_o", [L * 16, 128, 16, 128], "ExternalInput")
    w_cq_t = dram("w_cq", [L * 16, 128, 16, 128], "ExternalInput")
    w_ck_t = dram("w_ck", [L * 16, 128, 16, 128], "ExternalInput")
    w_cv_t = dram("w_cv", [L * 16, 128, 16, 128], "ExternalInput")
    w_cx_t = dram("w_co", [L * 16, 128, 16, 128], "ExternalInput")
    w_fg_t = dram("w_ffn_gate", [L * 44, 128, 16, 128], "ExternalInput")
    w_fu_t = dram("w_ffn_up", [L * 44, 128, 16, 128], "ExternalInput")
    w_fd_t = dram("w_ffn_down", [L * 16, 128, 44, 128], "ExternalInput")
    vecs_in = dram("vecs", [128, L * NVEC], "ExternalInput")
    mask_in = dram("masks", [128, 29 * 128], "ExternalInput")
    outT = dram("outT", [8, 128, 16 * 256], "ExternalOutput")
    xs = dram("xs_scratch", [8, 128, 16 * 256], "Internal")
    ys = dram("ys_scratch", [8, 128, 16 * 256], "Internal")

    S = Sched()
    st = contextlib.ExitStack()
    with st:
        arena = st.enter_context(nc.sbuf_tensor("arena", [128, ARENA_BYTES // 2], BF16))
        VEC = st.enter_context(nc.sbuf_tensor("vec", [128, L * NVEC], F32))
        ONES = st.enter_context(nc.sbuf_tensor("ones", [128, 128], BF16))
        IDENT = st.enter_context(nc.sbuf_tensor("ident", [128, 128], BF16))
        EPST = st.enter_context(nc.sbuf_tensor("epst", [128, 1], F32))
        PS = st.enter_context(nc.psum_tensor("ps", [128, 4096], F32))

        def bank(b, n=512, off=0):
            return PS[:, b * 512 + off: b * 512 + off + n]

        def pk(*bs):
            return [("ps", b) for b in bs]

        def av(off, nbytes, dt=BF16):
            lo = off // 2
            v = arena[:, lo: lo + nbytes // 2]
            if dt == F32:
                v = v.bitcast(F32)
            keys = [("A", k) for k in range(off // KB, (off + nbytes - 1) // KB + 1)]
            return v, keys

        HA = arena[:, 0: 32768].rearrange("p (c n) -> p c n", c=16)

        def ha_keys(t0=0, t1=SEQ):
            return [("HA", k) for k in range(t0 // 256, (t1 - 1) // 256 + 1)]

        ring = []
        for s_ in range(NSLOT):
            v, k = av(RING_OFF + s_ * 4 * KB, 4 * KB)
            ring.append((v.rearrange("p (k c) -> p k c", k=16), k))
        ring_ctr = [0]

        def load_piece(src):
            kcn = src.shape[1]
            slot = ring_ctr[0] % NSLOT
            ring_ctr[0] += 1
            v, k = ring[slot]
            dst = v[:, 0:kcn, :]
            S.op("pool", lambda e, dst=dst, src=src: e.dma_start(out=dst, in_=src), writes=k, dma=True)
            return dst, k

        dump_names = []

        def dump(name, view, keys, dt=BF16):
            if not _DUMP:
                return
            shp = list(view.shape)
            t = nc.dram_tensor("dump_" + name, shp, dt, kind="ExternalOutput").ap()
            dump_names.append("dump_" + name)
            S.op("sp", lambda e: e.dma_start(out=t, in_=view), reads=keys, writes=[("dump", name)], dma=True)

        S.op("sp", lambda e: e.dma_start(out=VEC[:], in_=vecs_in), writes=["vec"], dma=True)
        S.op("dve", lambda e: e.memset(ONES[:], 1.0), writes=["ones"])
        S.op("dve", lambda e: e.memset(EPST[:], EPS), writes=["eps"])
        S.op("pool", lambda e: e.dma_start(out=IDENT[:], in_=mask_in[:, 28 * 128:29 * 128]), writes=["ident"], dma=True)

        def vec(l, idx, c):
            o = l * NVEC + idx * 16 + c
            return VEC[:, o:o + 1]

        def cvec(l, idx, cc):
            o = l * NVEC + 112 + idx * 8 + cc
            return VEC[:, o:o + 1]

        def wdw(l, cc):
            o = l * NVEC + 136 + cc * 31
            return VEC[:, o:o + 31]

        G_MIX_PRE, G_MIX_POST, G_MEM, G_X_PRE, G_X_POST, G_FFN_PRE, G_FFN_POST = range(7)

        bank_rr = [0]

        def next_bank():
            b = bank_rr[0] % 8
            bank_rr[0] += 1
            return b

        def rstd_from_bank(b, n, dst, dkeys, scale):
            S.op("act", lambda e: e.activation(out=dst, in_=bank(b, n), func=AF.Sqrt, scale=scale, bias=EPST[:]),
                 reads=pk(b) + ["eps"], writes=dkeys)
            S.op("dve", lambda e: e.reciprocal(out=dst, in_=dst), reads=dkeys, writes=dkeys)

        NTU = 256
        NT_U = SEQ // NTU

        UP = (BIG_OFF + 40 * KB, BIG_OFF + 56 * KB, BIG_OFF + 72 * KB)
        LO = (BIG_OFF, BIG_OFF + 16 * KB, BIG_OFF + 32 * KB)
        TOP = (BIG_OFF + 48 * KB, BIG_OFF + 64 * KB, BIG_OFF + 80 * KB)

        def reg_disjoint(r1, r2):
            iv1 = [(r1[0], 16 * KB), (r1[1], 16 * KB), (r1[2], 8 * KB)]
            iv2 = [(r2[0], 16 * KB), (r2[1], 16 * KB), (r2[2], 8 * KB)]
            for (a, la) in iv1:
                for (b, lb) in iv2:
                    if a < b + lb and b < a + la:
                        return False
            return True

        def update_stages(l, x_src, has_y, g_post, x_dst, g_pre, final=False, l_pre=None, regs=None, add_eng="pool"):
            l_pre = l if l_pre is None else l_pre
            xdv = x_dst
            if regs is None:
                regs = [UP, LO] * 4

            def bufs(t):
                xo, yo, so = regs[t]
                XT, xk = av(xo, 16 * KB, F32)
                YT, yk = av(yo, 16 * KB, F32)
                SQ, sk = av(so, 8 * KB)
                return (XT.rearrange("p (c n) -> p c n", c=16), xk, YT.rearrange("p (c n) -> p c n", c=16), yk,
                        SQ.rearrange("p (c n) -> p c n", c=16), sk)

            RY, ryk = av(SCR_OFF + 26 * KB, KB, F32)
            RX, rxk = av(SCR_OFF + 27 * KB, KB, F32)

            def st_L(t):
                XT, xk, YT, yk, SQ, sk = bufs(t)
                half = 0 if t * NTU < 1024 else 1
                xsrc_keys = [("xs", t)] if x_src is xs else []
                S.op("sp", lambda e: e.dma_start(out=XT, in_=x_src[t].rearrange("p (c n) -> p c n", c=16)),
                     reads=xsrc_keys, writes=xk, dma=True)
                if has_y:
                    S.op("sp", lambda e: e.dma_start(out=YT, in_=ys[t].rearrange("p (c n) -> p c n", c=16)),
                         reads=[("ys", m, half) for m in range(16)], writes=yk, dma=True)

            def st_A(t):
                XT, xk, YT, yk, SQ, sk = bufs(t)
                if has_y:
                    S.op("act", lambda e: e.activation(out=SQ, in_=YT, func=AF.Square), reads=yk, writes=sk)
                else:
                    S.op("act", lambda e: e.activation(out=SQ, in_=XT, func=AF.Square), reads=xk, writes=sk)

            def st_B(t):
                XT, xk, YT, yk, SQ, sk = bufs(t)
                b = next_bank()

                def ssq(e):
                    for c in range(16):
                        i = e.matmul(bank(b, NTU), lhsT=ONES[:], rhs=SQ[:, c, :], start=(c == 0), stop=(c == 15))
                    return i
                S.op("pe", ssq, reads=sk + ["ones"], writes=pk(b))
                rstd_from_bank(b, NTU, RY, ryk, 1.0 / D)

                def scale_y(e):
                    for c in range(16):
                        i = e.scalar_tensor_tensor(out=YT[:, c, :], in0=YT[:, c, :], scalar=vec(l, g_post, c),
                                                   in1=RY, op0=ALU.mult, op1=ALU.mult)
                    return i
                S.op("dve", scale_y, reads=yk + ryk + ["vec"], writes=yk)
                S.op(add_eng(t) if callable(add_eng) else add_eng, lambda e: e.tensor_tensor(out=XT, in0=XT, in1=YT, op=ALU.add),
                     reads=xk + yk, writes=xk)
                if xdv is not None:
                    dkeys = [("xs", t)] if x_dst is xs else [("out", t)]
                    S.op("sp", lambda e: e.dma_start(out=xdv[t].rearrange("p (c n) -> p c n", c=16), in_=XT),
                         reads=xk, writes=dkeys, dma=True)
                if not final:
                    S.op("act", lambda e: e.activation(out=SQ, in_=XT, func=AF.Square), reads=xk, writes=sk)

            def st_C(t):
                XT, xk, YT, yk, SQ, sk = bufs(t)
                t0 = t * NTU
                b2 = next_bank()

                def ssq2(e):
                    for c in range(16):
                        i = e.matmul(bank(b2, NTU), lhsT=ONES[:], rhs=SQ[:, c, :], start=(c == 0), stop=(c == 15))
                    return i
                S.op("pe", ssq2, reads=sk + ["ones"], writes=pk(b2))
                rstd_from_bank(b2, NTU, RX, rxk, 1.0 / D)

                def mk_h(e):
                    for c in range(16):
                        i = e.scalar_tensor_tensor(out=HA[:, c, t0:t0 + NTU], in0=XT[:, c, :], scalar=vec(l_pre, g_pre, c),
                                                   in1=RX, op0=ALU.mult, op1=ALU.mult)
                    return i
                S.op("dve", mk_h, reads=xk + rxk + ["vec"], writes=ha_keys(t0, t0 + NTU))

            stages = {}
            for t in range(NT_U):
                mk = lambda f, t=t: (lambda: f(t))
                stages[t] = [mk(st_L), mk(st_A), (mk(st_B) if has_y else None), (None if final else mk(st_C))]
            return stages, regs

        def sched_tiles(ts, stages, regs, lat=(3, 1, 3, 1)):
            ts = list(ts)
            stage = {t: 0 for t in ts}
            ready = {t: 0 for t in ts}
            hooks = []
            h = 0
            while any(stage[t] < 4 for t in ts):
                cur = []
                for t in ts:
                    while stage[t] < 4:
                        s_ = stage[t]
                        if s_ == 0 and any(stage[u] < 4 and not reg_disjoint(regs[u], regs[t]) for u in ts if u < t):
                            break
                        if ready[t] > h:
                            break
                        fn = stages[t][s_]
                        stage[t] = s_ + 1
                        if fn is None:
                            continue
                        cur.append(fn)
                        ready[t] = h + lat[s_]
                        break
                hooks.append(cur)
                h += 1
                assert h < 1000
            return hooks

        def run_hook(hooks):
            if hooks:
                for fn in hooks.pop(0):
                    fn()

        def flush_hooks(hooks):
            while hooks:
                run_hook(hooks)

        def update_pass(*a, **kw):
            stages, regs = update_stages(*a, **kw)
            flush_hooks(sched_tiles(range(NT_U), stages, regs))

        def out_proj_to_ys(wt, l, kcn, rhs_fn, rhs_keys, halves, stage_off, bg=None, bg_hf=1):
            pair = [0]
            for hf in halves:
                for m in range(16):
                    if bg is not None and hf == bg_hf:
                        run_hook(bg)
                    pieces = []
                    for k0 in range(0, kcn, 16):
                        kn = min(16, kcn - k0)
                        pieces.append((k0, kn, load_piece(wt[l * 16 + m][:, k0:k0 + kn, :])))
                    b0 = (pair[0] % 4) * 2
                    pair[0] += 1
                    for (k0, kn, (wv, wk)) in pieces:
                        def mm(e, wv=wv, k0=k0, kn=kn, b0=b0, hf=hf):
                            for kc in range(kn):
                                for tt in range(2):
                                    i = e.matmul(bank(b0 + tt), lhsT=wv[:, kc, :],
                                                 rhs=rhs_fn(k0 + kc, hf * 1024 + tt * 512, 512),
                                                 start=(k0 + kc == 0), stop=(k0 + kc == kcn - 1))
                            return i
                        S.op("pe", mm, reads=wk + rhs_keys(hf), writes=pk(b0, b0 + 1))
                    sp_ = pair[0] % 2
                    YS, ysk = av(stage_off + sp_ * 4 * KB, 4 * KB, F32)
                    S.op("act", lambda e, YS=YS, b0=b0: e.activation(out=YS, in_=PS[:, b0 * 512:(b0 + 2) * 512], func=AF.Copy),
                         reads=pk(b0, b0 + 1), writes=ysk)
                    S.op("act", lambda e, YS=YS, m=m, hf=hf: e.dma_start(out=ys[4 * hf:4 * hf + 4, :, m * 256:(m + 1) * 256].rearrange("t p n -> p t n"), in_=YS.rearrange("p (t n) -> p t n", t=4)),
                         reads=ysk, writes=[("ys", m, hf)], dma=True)
            if bg is not None:
                flush_hooks(bg)

        def mixer(l, bg_wo=None):
            SCONV, sconv_k = av(BIG_OFF, 32 * KB)
            SCONV = SCONV.rearrange("p (c n) -> p c n", c=8)
            MERG, merg_k = av(BIG_OFF + 32 * KB, 32 * KB)
            MERG = MERG.rearrange("p (c n) -> p c n", c=16)
            AO, ao_k = av(BIG_OFF + 64 * KB, 16 * KB)
            AO = AO.rearrange("p (c n) -> p c n", c=4)
            MASK, mask_k = av(BIG_OFF + 80 * KB, 7 * KB)
            S.op("pool", lambda e: e.dma_start(out=MASK, in_=mask_in[:, 0:28 * 128]), writes=mask_k, dma=True)
            hk_all = ha_keys()

            def zproj(chunk, b0):
                wv, wk = load_piece(w_in_t[l * 84 + chunk])

                def mm(e):
                    for kc in range(16):
                        for tt in range(4):
                            i = e.matmul(bank(b0 + tt), lhsT=wv[:, kc, :], rhs=HA[:, kc, tt * 512:(tt + 1) * 512],
                                         start=(kc == 0), stop=(kc == 15))
                    return i
                S.op("pe", mm, reads=wk + hk_all, writes=pk(*range(b0, b0 + 4)))

            def half_zproj(wv, wk, hf, b0):
                def mm(e):
                    for kc in range(16):
                        for t2 in range(2):
                            i = e.matmul(bank(b0 + t2), lhsT=wv[:, kc, :],
                                         rhs=HA[:, kc, hf * 1024 + t2 * 512: hf * 1024 + (t2 + 1) * 512],
                                         start=(kc == 0), stop=(kc == 15))
                    return i
                S.op("pe", mm, reads=wk + ha_keys(hf * 1024, (hf + 1) * 1024), writes=pk(b0, b0 + 1))

            T0 = BIG_OFF
            QT = [av(T0 + i * 4 * KB, 4 * KB) for i in range(2)]
            KT = [av(T0 + 8 * KB + i * 4 * KB, 4 * KB) for i in range(2)]
            VT = [av(T0 + 16 * KB + i * 4 * KB, 4 * KB) for i in range(2)]
            PB = [av(T0 + 24 * KB + i * KB, KB) for i in range(6)]
            ACCO, acco_k = av(T0 + 32 * KB, 8 * KB, F32)
            ACCD, accd_k = av(T0 + 40 * KB, 8 * KB, F32)
            DIL = ((1, 16), (4, 4), (16, 1))
            heads = [(j, g) for j in range(4) for g in range(3)]

            def proj_items(n):
                j, g = heads[n]
                hd = 4 * g + j
                d, tpr = DIL[g]
                par = n % 2
                qT, qk = QT[par]
                kT, kk = KT[par]
                vT, vk = VT[par]
                state = {}
                qk_items = []
                for (chunk, dstT, dk, nm) in ((16 + hd, qT, qk, "q"), (28 + hd, kT, kk, "k")):
                    for hf in range(2):
                        def it(chunk=chunk, dstT=dstT, dk=dk, nm=nm, hf=hf):
                            if hf == 0:
                                state[nm] = load_piece(w_in_t[l * 84 + chunk])
                            wv, wk = state[nm]
                            half_zproj(wv, wk, hf, 5)
                            for t2 in range(2):
                                tt = 2 * hf + t2
                                n_i = 512 // d
                                dv = dstT.rearrange("p (r i) -> p r i", r=d)[:, :, n_i * tt: n_i * (tt + 1)]
                                sv = bank(5 + t2).rearrange("p (i r) -> p r i", r=d)
                                if t2 == 0:
                                    S.op("act", lambda e, dv=dv, sv=sv: e.activation(out=dv, in_=sv, func=AF.Copy),
                                         reads=pk(5 + t2), writes=dk)
                                else:
                                    S.op("dve", lambda e, dv=dv, sv=sv: e.tensor_copy(out=dv, in_=sv),
                                         reads=pk(5 + t2), writes=dk)
                        qk_items.append(it)
                v_items = []
                for tq in range(4):
                    def itv(tq=tq):
                        if tq == 0:
                            state["v"] = load_piece(w_in_t[l * 84 + 40 + hd])
                        wv_, wvk = state["v"]

                        def vmm(e):
                            for tl in range(4):
                                T = tq * 4 + tl
                                r = T // tpr
                                start = 128 * (T % tpr) * d + r
                                for kc in range(16):
                                    i = e.matmul(bank(7, 128, tl * 128),
                                                 lhsT=HA[:, kc, start: start + 127 * d + 1: d],
                                                 rhs=wv_[:, kc, :], start=(kc == 0), stop=(kc == 15))
                            return i
                        S.op("pe", vmm, reads=wvk + hk_all, writes=pk(7))
                        dstv = vT[:, tq * 512:(tq + 1) * 512]
                        if tq % 2 == 0:
                            S.op("dve", lambda e: e.tensor_copy(out=dstv, in_=bank(7)), reads=pk(7), writes=vk)
                        else:
                            S.op("act", lambda e: e.activation(out=dstv, in_=bank(7), func=AF.Copy), reads=pk(7), writes=vk)
                    v_items.append(itv)
                items = []
                for i in range(4):
                    items.append(qk_items[i])
                    items.append(v_items[i])
                return items

            pset = [0]

            def attn_items(n):
                j, g = heads[n]
                hd = 4 * g + j
                d, tpr = DIL[g]
                par = n % 2
                qT, qk = QT[par]
                kT, kk = KT[par]
                vT, vk = VT[par]
                vT3 = vT.rearrange("p (t c) -> p t c", t=16)
                offs = (0, -1, 1) if tpr > 1 else (0,)
                out = []
                for c in range(4):
                    st_ = {}

                    def s_item(c=c, st_=st_):
                        pb0 = 0 if pset[0] % 2 == 0 else 3
                        pset[0] += 1
                        rng_o = {}
                        for oi, o in enumerate(offs):
                            tls = [tl for tl in range(4) if 0 <= ((4 * c + tl) % tpr) + o < tpr]
                            if not tls:
                                continue
                            lo, hi = tls[0], tls[-1] + 1
                            rng_o[o] = (lo, hi, oi)
                            sbk = oi
                            P, pkk = PB[pb0 + oi]

                            def smm(e, o=o, lo=lo, hi=hi, sbk=sbk):
                                for tl in range(lo, hi):
                                    T = 4 * c + tl
                                    i = e.matmul(bank(sbk, 128, tl * 128), lhsT=kT[:, 128 * (T + o):128 * (T + o + 1)],
                                                 rhs=qT[:, 128 * T:128 * (T + 1)], start=True, stop=True)
                                return i
                            S.op("pe", smm, reads=qk + kk, writes=pk(sbk))
                            S.op("act", lambda e, P=P, sbk=sbk, lo=lo, hi=hi: e.activation(
                                out=P[:, lo * 128:hi * 128], in_=bank(sbk, (hi - lo) * 128, lo * 128),
                                func=AF.Exp, scale=float(128 ** -0.5)), reads=pk(sbk), writes=pkk)
                            mi = mask_index(hd, o)
                            mv = MASK[:, mi * 128:(mi + 1) * 128].unsqueeze(1).broadcast_to([128, hi - lo, 128])
                            S.op("dve", lambda e, P=P, lo=lo, hi=hi, mv=mv: e.tensor_tensor(
                                out=P[:, lo * 128:hi * 128].rearrange("p (t n) -> p t n", n=128),
                                in0=P[:, lo * 128:hi * 128].rearrange("p (t n) -> p t n", n=128),
                                in1=mv, op=ALU.mult), reads=pkk + mask_k, writes=pkk)
                        st_["rng_o"] = rng_o
                        st_["pb0"] = pb0

                    def pv_item(c=c, st_=st_):
                        rng_o = st_["rng_o"]
                        pb0 = st_["pb0"]
                        all_pk = []
                        for o in rng_o:
                            all_pk += PB[pb0 + rng_o[o][2]][1]

                        def pvmm(e):
                            for tl in range(4):
                                T = 4 * c + tl
                                vo = [o for o in rng_o if rng_o[o][0] <= tl < rng_o[o][1]]
                                for idx, o in enumerate(vo):
                                    P = PB[pb0 + rng_o[o][2]][0]
                                    i = e.matmul(bank(3, 128, tl * 128), lhsT=vT3[:, T + o, :],
                                                 rhs=P[:, tl * 128:(tl + 1) * 128],
                                                 start=(idx == 0), stop=(idx == len(vo) - 1))
                            os_ = list(rng_o.keys())
                            for idx, o in enumerate(os_):
                                lo, hi, oi = rng_o[o]
                                P = PB[pb0 + oi][0]
                                i = e.matmul(bank(4, (hi - lo) * 128, lo * 128), lhsT=ONES[:],
                                             rhs=P[:, lo * 128:hi * 128], start=(idx == 0), stop=(idx == len(os_) - 1))
                            return i
                        S.op("pe", pvmm, reads=all_pk + vk + ["ones"], writes=pk(3, 4))
                        if g == 0:
                            ov = ACCO[:, 512 * c:512 * (c + 1)]
                            dvw = ACCD[:, 512 * c:512 * (c + 1)]
                            S.op("act", lambda e: e.activation(out=ov, in_=bank(3), func=AF.Copy),
                                 reads=pk(3), writes=acco_k)
                            S.op("dve", lambda e: e.tensor_copy(out=dvw, in_=bank(4)),
                                 reads=pk(4), writes=accd_k)
                        else:
                            if g == 1:
                                ov = ACCO.rearrange("p (i r) -> p r i", r=4)[:, c, :]
                                dvw = ACCD.rearrange("p (i r) -> p r i", r=4)[:, c, :]
                                so = bank(3)
                                sd = bank(4)
                            else:
                                ov = ACCO.rearrange("p (i r) -> p r i", r=16)[:, 4 * c:4 * c + 4, :]
                                dvw = ACCD.rearrange("p (i r) -> p r i", r=16)[:, 4 * c:4 * c + 4, :]
                                so = bank(3).rearrange("p (t i) -> p t i", t=4)
                                sd = bank(4).rearrange("p (t i) -> p t i", t=4)
                            S.op("dve", lambda e: e.tensor_tensor(out=ov, in0=ov, in1=so, op=ALU.add),
                                 reads=pk(3) + acco_k, writes=acco_k)
                            S.op("dve", lambda e: e.tensor_tensor(out=dvw, in0=dvw, in1=sd, op=ALU.add),
                                 reads=pk(4) + accd_k, writes=accd_k)
                        if g == 2 and c == 3:
                            S.op("dve", lambda e: e.reciprocal(out=ACCD, in_=ACCD), reads=accd_k, writes=accd_k)
                            S.op("dve", lambda e: e.tensor_tensor(out=AO[:, j, :], in0=ACCO, in1=ACCD, op=ALU.mult),
                                 reads=acco_k + accd_k, writes=ao_k)
                    out.append((s_item, pv_item))
                return out

            for it in proj_items(0):
                it()
            for n in range(12):
                A = attn_items(n)
                P = proj_items(n + 1) if n + 1 < 12 else []
                for c in range(4):
                    if P:
                        P[2 * c]()
                    A[c][0]()
                    if P:
                        P[2 * c + 1]()
                    A[c][1]()

            if l == 0:
                dump("AO", AO, ao_k)
                dump("QT", QT[0][0], QT[0][1])
                dump("KT", KT[0][0], KT[0][1])
                dump("VT", VT[0][0], VT[0][1])
            T1 = BIG_OFF + 32 * KB
            SIG = [av(T1 + i * 4 * KB, 4 * KB) for i in range(2)]
            UPAD = [av(T1 + 8 * KB + i * 4352, 4352) for i in range(2)]
            DG = [av(T1 + 8 * KB + 2 * 4352 + i * 7936, 7936) for i in range(2)]
            for i in range(2):
                u, uk = UPAD[i]
                S.op("pool", lambda e, u=u: e.memset(u[:, 0:2176], 0.0), writes=uk)
            pair_rr = [0]

            def next_pair():
                b = (pair_rr[0] % 4) * 2
                pair_rr[0] += 1
                return b

            def conv_cc(cc):
                par = cc % 2
                u, uk = UPAD[par]
                dg, dgk = DG[par]
                dg3 = dg.rearrange("p (k c) -> p k c", k=31)
                for hf in range(2):
                    b0 = next_pair()

                    def cmm(e, hf=hf, b0=b0):
                        for k in range(31):
                            for t2 in range(2):
                                tt = 2 * hf + t2
                                i = e.matmul(bank(b0 + t2), lhsT=dg3[:, k, :], rhs=u[:, tt * 512 + k: tt * 512 + k + 512],
                                             start=(k == 0), stop=(k == 30))
                        return i
                    S.op("pe", cmm, reads=dgk + uk, writes=pk(b0, b0 + 1))
                    S.op("act", lambda e, hf=hf, b0=b0: e.activation(out=SCONV[:, cc, hf * 1024:(hf + 1) * 1024],
                                                                      in_=PS[:, b0 * 512:(b0 + 2) * 512],
                                                                      func=AF.Identity, bias=cvec(l, 0, cc)),
                         reads=pk(b0, b0 + 1) + ["vec"], writes=sconv_k)

            for cc in range(8):
                par = cc % 2
                sg, sgk = SIG[par]
                u, uk = UPAD[par]
                dg, dgk = DG[par]
                dg3 = dg.rearrange("p (k c) -> p k c", k=31)
                S.op("dve", lambda e, dg3=dg3, cc=cc: e.tensor_tensor(
                    out=dg3, in0=IDENT[:].unsqueeze(1).broadcast_to([128, 31, 128]),
                    in1=wdw(l, cc).unsqueeze(2).broadcast_to([128, 31, 128]), op=ALU.mult),
                    reads=["ident", "vec"], writes=dgk)
                gw, gwk = load_piece(w_in_t[l * 84 + 8 + cc])
                for hf in range(2):
                    b0 = next_pair()
                    half_zproj(gw, gwk, hf, b0)
                    S.op("act", lambda e, sg=sg, hf=hf, b0=b0: e.activation(out=sg[:, hf * 1024:(hf + 1) * 1024],
                                                                             in_=PS[:, b0 * 512:(b0 + 2) * 512], func=AF.Sigmoid),
                         reads=pk(b0, b0 + 1), writes=sgk)
                aw, awk = load_piece(w_in_t[l * 84 + cc])
                for hf in range(2):
                    b0 = next_pair()
                    half_zproj(aw, awk, hf, b0)
                    S.op("dve", lambda e, u=u, sg=sg, hf=hf, b0=b0: e.tensor_tensor(
                        out=u[:, 15 + hf * 1024: 15 + (hf + 1) * 1024], in0=PS[:, b0 * 512:(b0 + 2) * 512],
                        in1=sg[:, hf * 1024:(hf + 1) * 1024], op=ALU.mult), reads=pk(b0, b0 + 1) + sgk, writes=uk)
                if cc > 0:
                    conv_cc(cc - 1)
            conv_cc(7)
            SQC, sqck = av(SCR_OFF, 8 * KB)
            SQC = SQC.rearrange("p (c n) -> p c n", c=8)
            MEAN, mk_ = av(SCR_OFF + 8 * KB, 2 * KB, F32)
            RSTD, rk_ = av(SCR_OFF + 10 * KB, 2 * KB, F32)
            M2, m2k = av(SCR_OFF + 12 * KB, 2 * KB, F32)
            TMP = [av(SCR_OFF + 14 * KB + i * 2 * KB, 2 * KB, F32) for i in range(2)]
            for tt in range(4):
                tsl = slice(tt * 512, (tt + 1) * 512)
                S.op("act", lambda e, tsl=tsl: e.activation(out=SQC, in_=SCONV[:, :, tsl], func=AF.Square),
                     reads=sconv_k, writes=sqck)
                b1 = next_bank()
                b2 = next_bank()

                def st1(e, b1=b1, b2=b2, tsl=tsl):
                    for cc in range(8):
                        i = e.matmul(bank(b1), lhsT=ONES[:], rhs=SCONV[:, cc, tsl], start=(cc == 0), stop=(cc == 7))
                    for cc in range(8):
                        i = e.matmul(bank(b2), lhsT=ONES[:], rhs=SQC[:, cc, :], start=(cc == 0), stop=(cc == 7))
                    return i
                S.op("pe", st1, reads=sconv_k + sqck + ["ones"], writes=pk(b1, b2))
                S.op("act", lambda e, b1=b1: e.activation(out=MEAN, in_=bank(b1), func=AF.Copy, scale=1.0 / 1024),
                     reads=pk(b1), writes=mk_)
                S.op("dve", lambda e: e.tensor_tensor(out=M2, in0=MEAN, in1=MEAN, op=ALU.mult), reads=mk_, writes=m2k)
                S.op("dve", lambda e, b2=b2: e.scalar_tensor_tensor(out=RSTD, in0=bank(b2), scalar=1.0 / 1024, in1=M2,
                                                                    op0=ALU.mult, op1=ALU.subtract),
                     reads=pk(b2) + m2k, writes=rk_)
                S.op("act", lambda e: e.activation(out=RSTD, in_=RSTD, func=AF.Sqrt, bias=EPST[:]),
                     reads=rk_ + ["eps"], writes=rk_)
                S.op("dve", lambda e: e.reciprocal(out=RSTD, in_=RSTD), reads=rk_, writes=rk_)
                for cc in range(8):
                    tmp, tk = TMP[cc % 2]
                    S.op("pool", lambda e, tmp=tmp, cc=cc, tsl=tsl: e.tensor_tensor(out=tmp, in0=SCONV[:, cc, tsl], in1=MEAN,
                                                                                     op=ALU.subtract),
                         reads=sconv_k + mk_, writes=tk)
                    S.op("dve", lambda e, tmp=tmp: e.tensor_tensor(out=tmp, in0=tmp, in1=RSTD, op=ALU.mult),
                         reads=tk + rk_, writes=tk)
                    S.op("act", lambda e, tmp=tmp, cc=cc, tsl=tsl: e.activation(out=SCONV[:, cc, tsl], in_=tmp, func=AF.Silu,
                                                                                 scale=cvec(l, 1, cc), bias=cvec(l, 2, cc)),
                         reads=tk + ["vec"], writes=sconv_k)

            if l == 0:
                dump("SCONV", SCONV, sconv_k)
            SGA, sgak = av(SCR_OFF, 4 * KB, F32)
            SGB, sgbk = av(SCR_OFF + 4 * KB, 4 * KB, F32)
            TM, tmk = av(SCR_OFF + 8 * KB, 4 * KB, F32)
            T2, t2k = av(SCR_OFF + 12 * KB, 4 * KB, F32)
            pair = 0
            for hf in range(2):
                hsl = slice(hf * 1024, (hf + 1) * 1024)
                hk_h = ha_keys(hf * 1024, (hf + 1) * 1024)

                def half_proj(src, kcn, rhs3, rkeys, b0, hf=hf):
                    wv, wk = load_piece(src)

                    def mm(e):
                        for kc in range(kcn):
                            for tt in range(2):
                                i = e.matmul(bank(b0 + tt), lhsT=wv[:, kc, :],
                                             rhs=rhs3[:, kc, hf * 1024 + tt * 512: hf * 1024 + (tt + 1) * 512],
                                             start=(kc == 0), stop=(kc == kcn - 1))
                        return i
                    S.op("pe", mm, reads=wk + rkeys, writes=pk(b0, b0 + 1))

                for m in range(16):
                    half_proj(w_in_t[l * 84 + 52 + m], 16, HA, hk_h, 0)
                    S.op("act", lambda e: e.activation(out=SGA, in_=PS[:, 0:1024], func=AF.Sigmoid),
                         reads=pk(0, 1), writes=sgak)
                    half_proj(w_co_t[l * 16 + m], 8, SCONV, sconv_k, 2)
                    S.op("dve", lambda e: e.tensor_tensor(out=TM, in0=PS[:, 1024:2048], in1=SGA, op=ALU.mult),
                         reads=pk(2, 3) + sgak, writes=tmk)
                    half_proj(w_in_t[l * 84 + 68 + m], 16, HA, hk_h, 4)
                    S.op("act", lambda e: e.activation(out=SGB, in_=PS[:, 2048:3072], func=AF.Sigmoid),
                         reads=pk(4, 5), writes=sgbk)
                    half_proj(w_ap_t[l * 16 + m], 4, AO, ao_k, 6)
                    S.op("dve", lambda e: e.tensor_tensor(out=T2, in0=PS[:, 3072:4096], in1=SGB, op=ALU.mult),
                         reads=pk(6, 7) + sgbk, writes=t2k)
                    S.op("dve", lambda e, m=m: e.tensor_tensor(out=MERG[:, m, :], in0=TM, in1=T2, op=ALU.add),
                         reads=tmk + t2k, writes=merg_k)
                out_proj_to_ys(w_o_t, l, 16, lambda kc, t0, n, hf=hf: MERG[:, kc, t0 - hf * 1024: t0 - hf * 1024 + n],
                               lambda hf_: merg_k, [hf], SCR_OFF + 16 * KB, bg=(bg_wo if hf == 1 else None), bg_hf=1)

        def cross(l, bg=None, bg_co=None):
            QX, qxk = av(BIG_OFF + 16 * KB, 64 * KB)
            QX = QX.rearrange("p (c n) -> p c n", c=16)
            KX, kxk = av(BIG_OFF, 8 * KB)
            KX = KX.rearrange("p (c n) -> p c n", c=16)
            VX, vxk = av(BIG_OFF + 8 * KB, 8 * KB)
            VX = VX.rearrange("p (t f) -> p t f", t=2)
            MEMN, mnk = av(BIG_OFF + 80 * KB, 8 * KB)
            MEMN = MEMN.rearrange("p (c n) -> p c n", c=16)
            MEMF, mfk = av(SCR_OFF, 16 * KB, F32)
            MEMF = MEMF.rearrange("p (c n) -> p c n", c=16)
            MSQ, msk = av(SCR_OFF + 16 * KB, 8 * KB)
            MSQ = MSQ.rearrange("p (c n) -> p c n", c=16)
            RM, rmk = av(SCR_OFF + 24 * KB, KB, F32)
            hk_all = ha_keys()
            S.op("sp", lambda e: e.dma_start(out=MEMF, in_=memT_in.rearrange("(c p) n -> p c n", p=128)), writes=mfk, dma=True)
            S.op("act", lambda e: e.activation(out=MSQ, in_=MEMF, func=AF.Square), reads=mfk, writes=msk)
            b = next_bank()

            def ssq(e, b=b):
                for c in range(16):
                    i = e.matmul(bank(b, 256), lhsT=ONES[:], rhs=MSQ[:, c, :], start=(c == 0), stop=(c == 15))
                return i
            S.op("pe", ssq, reads=msk + ["ones"], writes=pk(b))
            rstd_from_bank(b, 256, RM, rmk, 1.0 / D)

            def mkm(e):
                for c in range(16):
                    i = e.scalar_tensor_tensor(out=MEMN[:, c, :], in0=MEMF[:, c, :], scalar=vec(l, G_MEM, c), in1=RM,
                                               op0=ALU.mult, op1=ALU.mult)
                return i
            S.op("dve", mkm, reads=mfk + rmk + ["vec"], writes=mnk)
            for m in range(16):
                run_hook(bg)
                wv, wk = load_piece(w_ck_t[l * 16 + m])
                b = next_bank()

                def kmm(e, wv=wv, b=b):
                    for kc in range(16):
                        i = e.matmul(bank(b, 256), lhsT=wv[:, kc, :], rhs=MEMN[:, kc, :], start=(kc == 0), stop=(kc == 15))
                    return i
                S.op("pe", kmm, reads=wk + mnk, writes=pk(b))
                S.op("act", lambda e, m=m, b=b: e.activation(out=KX[:, m, :], in_=bank(b, 256), func=AF.Copy),
                     reads=pk(b), writes=kxk)
            for m in range(16):
                run_hook(bg)
                wv, wk = load_piece(w_cv_t[l * 16 + m])
                b = next_bank()

                def vmm(e, wv=wv, b=b):
                    for mt in range(2):
                        for kc in range(16):
                            i = e.matmul(bank(b, 128, mt * 128), lhsT=MEMN[:, kc, mt * 128:(mt + 1) * 128], rhs=wv[:, kc, :],
                                         start=(kc == 0), stop=(kc == 15))
                    return i
                S.op("pe", vmm, reads=wk + mnk, writes=pk(b))
                S.op("dve", lambda e, m=m, b=b: e.tensor_copy(out=VX[:, :, m * 128:(m + 1) * 128],
                                                              in_=bank(b, 256).rearrange("p (t f) -> p t f", t=2)),
                     reads=pk(b), writes=vxk)
            flush_hooks(bg)
            for m in range(16):
                wv, wk = load_piece(w_cq_t[l * 16 + m])
                b0 = (m % 2) * 4

                def qmm(e, wv=wv, b0=b0):
                    for kc in range(16):
                        for tt in range(4):
                            i = e.matmul(bank(b0 + tt), lhsT=wv[:, kc, :], rhs=HA[:, kc, tt * 512:(tt + 1) * 512],
                                         start=(kc == 0), stop=(kc == 15))
                    return i
                S.op("pe", qmm, reads=wk + hk_all, writes=pk(*range(b0, b0 + 4)))
                for tt in range(4):
                    if tt % 2 == 0:
                        S.op("act", lambda e, m=m, tt=tt, b0=b0: e.activation(out=QX[:, m, tt * 512:(tt + 1) * 512], in_=bank(b0 + tt),
                                                                               func=AF.Copy), reads=pk(b0 + tt), writes=qxk)
                    else:
                        S.op("dve", lambda e, m=m, tt=tt, b0=b0: e.tensor_copy(out=QX[:, m, tt * 512:(tt + 1) * 512], in_=bank(b0 + tt)),
                             reads=pk(b0 + tt), writes=qxk)
            PX = [av(SCR_OFF + i * KB, KB) for i in range(4)]
            RD, rdk = av(SCR_OFF + 4 * KB, 2 * KB, F32)
            iters = [(hx, qt) for hx in range(4) for qt in range(4)]

            def x_scores(i):
                hx, qt = iters[i]
                qsl = slice(qt * 512, (qt + 1) * 512)
                sb = (i % 2) * 2

                def smm(e):
                    for mt in range(2):
                        for dc in range(4):
                            ii = e.matmul(bank(sb + mt), lhsT=KX[:, 4 * hx + dc, mt * 128:(mt + 1) * 128], rhs=QX[:, 4 * hx + dc, qsl],
                                          start=(dc == 0), stop=(dc == 3))
                    return ii
                S.op("pe", smm, reads=kxk + qxk, writes=pk(sb, sb + 1))

            def x_rest(i):
                hx, qt = iters[i]
                qsl = slice(qt * 512, (qt + 1) * 512)
                sb = (i % 2) * 2
                pp = (i % 2) * 2
                for mt in range(2):
                    P, pkk = PX[pp + mt]
                    S.op("act", lambda e, P=P, mt=mt: e.activation(out=P, in_=bank(sb + mt), func=AF.Exp, scale=float(512 ** -0.5)),
                         reads=pk(sb + mt), writes=pkk)
                p0, p0k = PX[pp]
                p1, p1k = PX[pp + 1]

                def pv(e):
                    for mt, P in ((0, p0), (1, p1)):
                        ii = e.matmul(bank(sb), lhsT=ONES[:], rhs=P, start=(mt == 0), stop=(mt == 1))
                    for dc in range(4):
                        for mt, P in ((0, p0), (1, p1)):
                            ii = e.matmul(bank(4 + dc), lhsT=VX[:, mt, (4 * hx + dc) * 128:(4 * hx + dc + 1) * 128], rhs=P,
                                          start=(mt == 0), stop=(mt == 1))
                    return ii
                S.op("pe", pv, reads=p0k + p1k + vxk + ["ones"], writes=pk(sb, 4, 5, 6, 7))
                S.op("dve", lambda e: e.reciprocal(out=RD, in_=bank(sb)), reads=pk(sb), writes=rdk)
                for dc in range(4):
                    S.op("dve", lambda e, dc=dc: e.tensor_tensor(out=HA[:, 4 * hx + dc, qsl], in0=bank(4 + dc), in1=RD, op=ALU.mult),
                         reads=pk(4 + dc) + rdk, writes=ha_keys(qsl.start, qsl.stop))

            x_scores(0)
            for i in range(16):
                if i + 1 < 16:
                    x_scores(i + 1)
                x_rest(i)
            out_proj_to_ys(w_cx_t, l, 16, lambda kc, t0, n: HA[:, kc, t0:t0 + n],
                           lambda hf_: ha_keys(hf_ * 1024, (hf_ + 1) * 1024), [0, 1], SCR_OFF + 8 * KB, bg=bg_co, bg_hf=1)

        def ffn(l, bg0=None, bg1=None):
            ACTH, ahk = av(BIG_OFF, 88 * KB)
            ACTH = ACTH.rearrange("p (c n) -> p c n", c=44)
            SG = [av(SCR_OFF + i * 4 * KB, 4 * KB, F32) for i in range(2)]
            for hf in range(2):
                hk_h = ha_keys(hf * 1024, (hf + 1) * 1024)
                for f in range(44):
                    sg, sgk = SG[f % 2]
                    b0 = (f % 2) * 4
                    run_hook(bg0 if hf == 0 else bg1)
                    for (wt, bb) in ((w_fg_t, b0), (w_fu_t, b0 + 2)):
                        wv, wk = load_piece(wt[l * 44 + f])

                        def mm(e, wv=wv, bb=bb, hf=hf):
                            for kc in range(16):
                                for tt in range(2):
                                    i = e.matmul(bank(bb + tt), lhsT=wv[:, kc, :],
                                                 rhs=HA[:, kc, hf * 1024 + tt * 512: hf * 1024 + (tt + 1) * 512],
                                                 start=(kc == 0), stop=(kc == 15))
                            return i
                        S.op("pe", mm, reads=wk + hk_h, writes=pk(bb, bb + 1))
                    S.op("act", lambda e, sg=sg, b0=b0: e.activation(out=sg, in_=PS[:, b0 * 512:(b0 + 2) * 512], func=AF.Silu),
                         reads=pk(b0, b0 + 1), writes=sgk)
                    fk = [("A", k) for k in range((BIG_OFF + f * 2048) // KB, (BIG_OFF + f * 2048 + 2047) // KB + 1)]
                    S.op("dve", lambda e, sg=sg, b0=b0, f=f: e.tensor_tensor(out=ACTH[:, f, :], in0=PS[:, (b0 + 2) * 512:(b0 + 4) * 512], in1=sg,
                                                                              op=ALU.mult),
                         reads=pk(b0 + 2, b0 + 3) + sgk, writes=fk)
                if (bg0 if hf == 0 else bg1) is not None:
                    flush_hooks(bg0 if hf == 0 else bg1)
                out_proj_to_ys(w_fd_t, l, 44, lambda kc, t0, n, hf=hf: ACTH[:, kc, t0 - hf * 1024: t0 - hf * 1024 + n],
                               lambda hf_: ahk, [hf], SCR_OFF + 8 * KB)

        update_pass(0, xT_in, False, None, None, G_MIX_PRE)
        dump("HA0", HA, ha_keys())
        x_cur = xT_in
        stop = dbg
        done = False
        for l in range(L):
            if l > 0:
                S.new_epoch()
            if stop == (l, "mixer"):
                mixer(l)
                update_pass(l, x_cur, True, G_MIX_POST, outT, None, final=True)
                done = True
                break
            last = (l == L - 1)
            A1 = (BIG_OFF, BIG_OFF + 16 * KB, BIG_OFF + 64 * KB)
            U1B = (BIG_OFF + 32 * KB, BIG_OFF + 48 * KB, BIG_OFF + 72 * KB)
            st1, rg1 = update_stages(l, x_cur, True, G_MIX_POST, xs, G_X_PRE, regs=[A1] * 4 + [U1B] * 4, add_eng="dve")
            h1a = sched_tiles(range(0, 4), st1, rg1, lat=(2, 1, 2, 1))
            h1b = sched_tiles(range(4, 8), st1, rg1, lat=(4, 2, 4, 2))
            mixer(l, bg_wo=h1a)
            x_cur = xs
            if stop == (l, "cross"):
                cross(l, bg=h1b)
                update_pass(l, xs, True, G_X_POST, outT, None, final=True)
                done = True
                break
            st2, rg2 = update_stages(l, xs, True, G_X_POST, xs, G_FFN_PRE, regs=[UP, LO, UP, LO] + [TOP] * 4, add_eng="dve")
            h2a = sched_tiles(range(0, 4), st2, rg2)
            h2b = sched_tiles(range(4, 8), st2, rg2, lat=(2, 1, 1, 1))
            assert len(h2b) <= 23, len(h2b)
            cross(l, bg=h1b, bg_co=h2a)
            st3, rg3 = update_stages(l, xs, True, G_FFN_POST, (outT if last else xs), (None if last else G_MIX_PRE),
                                     final=last, l_pre=l + 1, regs=[TOP] * 4 + [UP, LO, UP, LO],
                                     add_eng=(lambda t: "dve" if t < 4 else "pool"))
            h3a = sched_tiles(range(0, 4), st3, rg3, lat=(2, 1, 1, 1))
            assert len(h3a) <= 23, len(h3a)
            u3_bg = (OVL_U3 & 2) if last else (OVL_U3 & 1)
            if u3_bg:
                ffn(l, bg0=h2b, bg1=h3a)
            else:
                ffn(l, bg0=h2b)
                flush_hooks(h3a)
            flush_hooks(sched_tiles(range(4, 8), st3, rg3))
        S.op("sp", lambda e: e.nop(), reads=[("out", t) for t in range(NT_U)])
        S.emit(nc, st)
    return nc


def _tile_w(w, kc):
    Lw, K, N = w.shape
    t = w.reshape(Lw, K // 128, 128, N // 128, 128).transpose(0, 3, 2, 1, 4)
    return np.ascontiguousarray(t).reshape(Lw * (N // 128), 128, K // 128, 128)


def _alibi_masks():
    slopes = np.exp2(-8.0 * (np.arange(12, dtype=np.float64) + 1.0) / 12)
    kk = np.arange(128)[:, None]
    qq = np.arange(128)[None, :]
    out = np.zeros((128, 29 * 128), np.float32)
    out[:, 28 * 128:] = np.eye(128, dtype=np.float32)
    dil = (1, 4, 16)
    for hd in range(12):
        g = hd // 4
        for o in ((-1, 0, 1) if g < 2 else (0,)):
            rel = 128 * o + kk - qq
            m = np.where(np.abs(rel) <= 64, np.exp(-slopes[hd] * dil[g] * np.abs(rel)), 0.0)
            mi = mask_index(hd, o)
            out[:, mi * 128:(mi + 1) * 128] = m.astype(np.float32)
    return out


def _pack_vecs(inp, L):
    v = np.zeros((128, L * NVEC), np.float32)
    names = ["g_mix_pre", "g_mix_post", "g_mem", "g_x_pre", "g_x_post", "g_ffn_pre", "g_ffn_post"]
    for l in range(L):
        base = l * NVEC
        for i, n in enumerate(names):
            v[:, base + i * 16: base + (i + 1) * 16] = np.asarray(inp[n][l], np.float32).reshape(16, 128).T
        for i, n in enumerate(["b_dw", "conv_ln_g", "conv_ln_b"]):
            v[:, base + 112 + i * 8: base + 112 + (i + 1) * 8] = np.asarray(inp[n][l], np.float32).reshape(8, 128).T
        wd = np.asarray(inp["w_dw"][l], np.float32)
        v[:, base + 136: base + 136 + 248] = wd.T.reshape(8, 128, 31).transpose(1, 0, 2).reshape(128, 248)
    return v


_NC_CACHE = {}
_DBG = None
OVL_WO = True
OVL_CO = True
OVL_U3 = 3
_DUMP = False
_DUMP_RES = {}


def kernel(**inputs):
    L = DEPTH
    f = lambda k: np.asarray(inputs[k], np.float32)
    shared = {
        "w_in": _tile_w(f("w_in"), 16),
        "w_conv_out": _tile_w(f("w_conv_out"), 8),
        "w_attn_proj": _tile_w(f("w_attn_proj"), 4),
        "w_o": _tile_w(f("w_o"), 16),
        "w_cq": _tile_w(f("w_cq"), 16),
        "w_ck": _tile_w(f("w_ck"), 16),
        "w_cv": _tile_w(f("w_cv"), 16),
        "w_co": _tile_w(f("w_co"), 16),
        "w_ffn_gate": _tile_w(f("w_ffn_gate"), 16),
        "w_ffn_up": _tile_w(f("w_ffn_up"), 16),
        "w_ffn_down": _tile_w(f("w_ffn_down"), 44),
        "vecs": _pack_vecs(inputs, L),
        "masks": _alibi_masks(),
    }
    x = f("x")
    mem = f("mem")
    in_maps = []
    for b in range(8):
        m = dict(shared)
        m["xT"] = np.ascontiguousarray(x[b].reshape(8, 256, 16, 128).transpose(0, 3, 2, 1)).reshape(8, 128, 4096)
        m["memT"] = np.ascontiguousarray(mem[b].T)
        in_maps.append(m)
    if "nc" not in _NC_CACHE:
        _NC_CACHE["nc"] = build_program(L, _DBG)
    res = run_bass_kernel_spmd(_NC_CACHE["nc"], in_maps, core_ids=list(range(8)))
    if _DUMP:
        for k_ in res.results[0]:
            if k_.startswith("dump_"):
                _DUMP_RES[k_] = np.asarray(res.results[0][k_])
    out = np.stack([np.ascontiguousarray(np.asarray(res.results[b]["outT"]).reshape(8, 128, 16, 256).transpose(0, 3, 2, 1)).reshape(SEQ, D)
                    for b in range(8)], axis=0)
    return out.astype(np.float32)
```
